# Optimizing a Trainium2 kernel written in Bass

```python
import jax, jax.numpy as jnp
from jax import lax
import numpy as np

D_MODEL = 1024
BATCH = 8
SEQ = 8192
DEPTH = 2
DEC_BATCH = 16
DEC_SEQ = 4096
PAST_LEN = 128

HEAD_DIM = 64
N_Q_HEADS = 8
N_KV_HEADS = 2
Q_PER_KV = N_Q_HEADS // N_KV_HEADS
ATTN_WIDTH = N_Q_HEADS * HEAD_DIM
KV_WIDTH = N_KV_HEADS * HEAD_DIM
CONV_GROUPS = 8
CONV_WIDTH = CONV_GROUPS * HEAD_DIM
CONV_K = 3
MIX_WIDTH = ATTN_WIDTH + CONV_WIDTH
IN_PROJ_WIDTH = ATTN_WIDTH + 2 * KV_WIDTH + 3 * CONV_WIDTH
D_FF = ((8 * D_MODEL // 3 + 255) // 256) * 256
GRID_W = 64
ROPE_THETA = 10000.0
ROPE_PAIRS_PER_AXIS = HEAD_DIM // 4
Q_BLOCK = 128
N_MOD = 6
EPS = 1e-6

kernel_name = "hymba_attn_shortconv_adaln_encoder"


def rmsnorm(x, g):
    xf = x.astype(jnp.float32)
    y = xf * lax.rsqrt(jnp.mean(xf * xf, axis=-1, keepdims=True) + EPS)
    return (y * g.astype(jnp.float32)).astype(x.dtype)


def axial_rotary_tables(S):
    rows = S // GRID_W
    row = jnp.repeat(jnp.arange(rows, dtype=jnp.float32), GRID_W)
    col = jnp.tile(jnp.arange(GRID_W, dtype=jnp.float32), rows)
    inv = ROPE_THETA ** (-jnp.arange(ROPE_PAIRS_PER_AXIS, dtype=jnp.float32) / ROPE_PAIRS_PER_AXIS)
    ang = jnp.concatenate([row[:, None] * inv, col[:, None] * inv], axis=-1)
    return jnp.cos(ang), jnp.sin(ang)


def apply_rotary(x, cos, sin):
    B, S, H, D = x.shape
    xp = x.astype(jnp.float32).reshape(B, S, H, D // 2, 2)
    x0, x1 = xp[..., 0], xp[..., 1]
    c = cos[None, :, None, :]
    s = sin[None, :, None, :]
    out = jnp.stack([x0 * c - x1 * s, x0 * s + x1 * c], axis=-1)
    return out.reshape(B, S, H, D).astype(x.dtype)


def gqa_bidirectional(q, k, v):
    B, S, _, _ = q.shape
    nblk = S // Q_BLOCK
    scale = HEAD_DIM ** -0.5
    qb = q.reshape(B, nblk, Q_BLOCK, N_KV_HEADS, Q_PER_KV, HEAD_DIM).transpose(1, 0, 2, 3, 4, 5)

    def block(qblk):
        s = jnp.einsum("bqkgd,bskd->bkgqs", qblk, k).astype(jnp.float32) * scale
        p = jax.nn.softmax(s, axis=-1)
        return jnp.einsum("bkgqs,bskd->bqkgd", p.astype(v.dtype), v)

    o = lax.map(block, qb)
    return o.transpose(1, 0, 2, 3, 4, 5).reshape(B, S, ATTN_WIDTH)


def centred_short_conv(u, w):
    up = jnp.pad(u, ((0, 0), (1, 1), (0, 0)))
    return up[:, :-2] * w[0] + up[:, 1:-1] * w[1] + up[:, 2:] * w[2]


def encoder_layer(x, c, w_mod, b_mod, g_mix, w_in, q_gain, k_gain, conv_w, w_out,
                  g_ffn, w_gate, w_up, w_down):
    B, S, _ = x.shape
    mod = (jax.nn.silu(c) @ w_mod + b_mod)[:, None, :]
    shift1, scale1, gate1, shift2, scale2, gate2 = jnp.split(mod, N_MOD, axis=-1)

    h = rmsnorm(x, g_mix) * (1 + scale1) + shift1
    proj = h @ w_in
    splits = np.cumsum([ATTN_WIDTH, KV_WIDTH, KV_WIDTH, CONV_WIDTH, CONV_WIDTH]).tolist()
    q, k, v, gb, gc, u = jnp.split(proj, splits, axis=-1)

    cos, sin = axial_rotary_tables(S)
    q = rmsnorm(q.reshape(B, S, N_Q_HEADS, HEAD_DIM), q_gain)
    k = rmsnorm(k.reshape(B, S, N_KV_HEADS, HEAD_DIM), k_gain)
    q = apply_rotary(q, cos, sin)
    k = apply_rotary(k, cos, sin)
    v = v.reshape(B, S, N_KV_HEADS, HEAD_DIM)
    attn = gqa_bidirectional(q, k, v)

    conv = gb * centred_short_conv(gc * u, conv_w)

    x = x + gate1 * (jnp.concatenate([attn, conv], axis=-1) @ w_out)

    h = rmsnorm(x, g_ffn) * (1 + scale2) + shift2
    f = (jax.nn.silu(h @ w_gate) * (h @ w_up)) @ w_down
    return x + gate2 * f


def setup_inputs(seed: int = 0) -> dict:
    key = jax.random.key(seed)
    ks = jax.random.split(key, 20)
    f32 = jnp.float32
    D = D_MODEL

    def nrm(k, shape, scale):
        return jax.random.normal(k, shape, f32) * scale

    return {
        "x_prompt": nrm(ks[0], (BATCH, SEQ, D), 1.0),
        "x_sample": nrm(ks[1], (DEC_BATCH, DEC_SEQ, D), 1.0),
        "c_prompt": nrm(ks[2], (BATCH, D), 1.0),
        "c_sample": nrm(ks[3], (DEC_BATCH, D), 1.0),
        "w_mod": nrm(ks[4], (DEPTH, D, N_MOD * D), D ** -0.5),
        "b_mod": nrm(ks[5], (DEPTH, N_MOD * D), 0.01),
        "g_mix": 1.0 + nrm(ks[6], (DEPTH, D), 0.01),
        "w_in": nrm(ks[7], (DEPTH, D, IN_PROJ_WIDTH), D ** -0.5),
        "q_gain": 1.0 + nrm(ks[8], (DEPTH, HEAD_DIM), 0.01),
        "k_gain": 1.0 + nrm(ks[9], (DEPTH, HEAD_DIM), 0.01),
        "conv_w": nrm(ks[10], (DEPTH, CONV_K, CONV_WIDTH), CONV_K ** -0.5),
        "w_out": nrm(ks[11], (DEPTH, MIX_WIDTH, D), MIX_WIDTH ** -0.5),
        "g_ffn": 1.0 + nrm(ks[12], (DEPTH, D), 0.01),
        "w_gate": nrm(ks[13], (DEPTH, D, D_FF), D ** -0.5),
        "w_up": nrm(ks[14], (DEPTH, D, D_FF), D ** -0.5),
        "w_down": nrm(ks[15], (DEPTH, D_FF, D), D_FF ** -0.5),
        "g_final": 1.0 + nrm(ks[16], (D,), 0.01),
    }


def reference(x_prompt, x_sample, c_prompt, c_sample, w_mod, b_mod, g_mix, w_in, q_gain,
              k_gain, conv_w, w_out, g_ffn, w_gate, w_up, w_down, g_final):
    hp = x_prompt
    hs = x_sample
    for l in range(DEPTH):
        hp = encoder_layer(hp, c_prompt, w_mod[l], b_mod[l], g_mix[l], w_in[l], q_gain[l],
                           k_gain[l], conv_w[l], w_out[l], g_ffn[l], w_gate[l], w_up[l], w_down[l])
        hs = encoder_layer(hs, c_sample, w_mod[l], b_mod[l], g_mix[l], w_in[l], q_gain[l],
                           k_gain[l], conv_w[l], w_out[l], g_ffn[l], w_gate[l], w_up[l], w_down[l])
    y_prompt = rmsnorm(hp, g_final)
    y_sample = rmsnorm(hs, g_final)
    return (y_prompt, y_sample)
```

```python
import math
from contextlib import ExitStack
import numpy as np
import concourse.bass as bass
import concourse.mybir as mybir
from concourse.bass_utils import run_bass_kernel_spmd

F32 = mybir.dt.float32
BF16 = mybir.dt.bfloat16
AF = mybir.ActivationFunctionType
ALU = mybir.AluOpType

D = 1024
KC = 8
L = 2
DFF = 2816
FC = 22
NIN = 2304
EPS = 1e-6
NCORES = 8
HD = 64

V_CT = 0
V_BMOD = 24
V_GMIX = 120
V_GFFN = 136
V_GFIN = 152
V_QG = 160
V_QGS = 162
V_KG = 164
V_KGS = 166
V_CONV = 168
V_JF = 192
V_AXIS = 193
V_SGN = 194
NV = 196


class Sem:
    def __init__(self, nc, es, name):
        self.h = es.enter_context(nc.semaphore(name))
        self.cnt = 0
        self.name = name


class Eng:
    def __init__(self, nc, es, e, name):
        self.e = e
        self.sem = Sem(nc, es, "s_" + name)
        self.seen = {}

    def wait(self, toks):
        for (sem, v) in toks:
            if self.seen.get(sem, 0) >= v:
                continue
            self.e.wait_ge(sem.h, v)
            self.seen[sem] = v


class Res:
    __slots__ = ("w", "r")

    def __init__(self):
        self.w = None
        self.r = {}


def _deps(eng, reads, writes):
    toks = []
    for r in reads:
        if r.w is not None:
            toks.append(r.w)
    for w in writes:
        if w.w is not None:
            toks.append(w.w)
        for s, v in w.r.items():
            if s is not eng.sem:
                toks.append((s, v))
    eng.wait(toks)


def _commit(tok, reads, writes):
    for r in reads:
        if r.r.get(tok[0], 0) < tok[1]:
            r.r[tok[0]] = tok[1]
    for w in writes:
        w.w = tok
        w.r = {}


def op(eng, fn, reads=(), writes=()):
    _deps(eng, reads, writes)
    ins = fn()
    eng.sem.cnt += 1
    ins.then_inc(eng.sem.h, 1)
    tok = (eng.sem, eng.sem.cnt)
    _commit(tok, reads, writes)
    return tok


def dma(q, sem, out, in_, reads=(), writes=()):
    _deps(q, reads, writes)
    ins = q.e.dma_start(out=out, in_=in_)
    sem.cnt += 16
    ins.then_inc(sem.h, 16)
    tok = (sem, sem.cnt)
    _commit(tok, reads, writes)
    return tok


def run_all(g):
    for _ in g:
        pass

def interleave(ga, gb, nb=1):
    da = db = False
    while not (da and db):
        if not da and next(ga, "end") == "end":
            da = True
        for _ in range(nb):
            if not db and next(gb, "end") == "end":
                db = True


def build(seqs):
    NTOK = sum(seqs)
    SMAX = max(seqs)
    NS = len(seqs)
    offs = [sum(seqs[:i]) for i in range(NS)]
    nc = bass.Bass("TRN2", target_bir_lowering=False)

    def din(name, shape, dt=F32):
        return nc.dram_tensor(name, shape, dt, kind="ExternalInput").ap()

    xT_d = din("xT", [D, NTOK])
    vecs_d = din("vecs", [128, NV])
    pos_d = din("pos", [128, 2, SMAX])
    perm_d = din("perm", [128, 128])
    wmod_d = din("w_mod", [L, D, 6 * D])
    win_d = din("w_in", [L, D, NIN])
    wout_d = din("w_out", [L, D, D])
    wg_d = din("w_gate", [L, D, DFF])
    wu_d = din("w_up", [L, D, DFF])
    wd_d = din("w_down", [L, DFF, D])
    yT_d = nc.dram_tensor("yT", [D, NTOK], F32, kind="ExternalOutput").ap()
    xA_d = nc.dram_tensor("xA", [D, NTOK], F32, kind="Internal").ap()
    xB_d = nc.dram_tensor("xB", [D, NTOK], F32, kind="Internal").ap()
    qS_d = nc.dram_tensor("qS", [8, HD, NTOK], BF16, kind="Internal").ap()
    cS_d = nc.dram_tensor("cS", [512, NTOK], BF16, kind="Internal").ap()
    cos_d = nc.dram_tensor("cosT", [128, SMAX], F32, kind="Internal").ap()
    sin_d = nc.dram_tensor("sinT", [128, SMAX], F32, kind="Internal").ap()

    es = ExitStack()
    PE = Eng(nc, es, nc.tensor, "pe")
    ACT = Eng(nc, es, nc.scalar, "act")
    DVE = Eng(nc, es, nc.vector, "dve")
    POOL = Eng(nc, es, nc.gpsimd, "pool")
    SP = Eng(nc, es, nc.sync, "sp")
    engines = [PE, ACT, DVE, POOL, SP]
    all_sems = [e.sem for e in engines]
    sem_ctr = [0]

    sem_pool = {}

    def newsem(name):
        if name in sem_pool:
            return sem_pool[name]
        sem_ctr[0] += 1
        s = Sem(nc, es, "d%d_%s" % (sem_ctr[0], name))
        all_sems.append(s)
        sem_pool[name] = s
        return s

    def barrier():
        toks = [(s, s.cnt) for s in all_sems if s.cnt > 0]
        for e in engines:
            e.wait(toks)

    sb_ctr = [0]

    def sb(stack, name, shape, dt):
        sb_ctr[0] += 1
        return stack.enter_context(nc.sbuf_tensor("sb%d_%s" % (sb_ctr[0], name), shape, dt))

    psP = [es.enter_context(nc.psum_tensor("psP%d" % i, [128, 1024], F32)) for i in range(4)]
    ps = []
    for i in range(4):
        ps.append(psP[i][:, 0:512])
        ps.append(psP[i][:, 512:1024])
    psR = [Res() for _ in range(8)]

    vecs = sb(es, "vecs", [128, NV], F32)
    modT = sb(es, "modT", [128, L, 48, NS], F32)
    A1 = sb(es, "A1", [128, L, NS, 8], F32)
    A2 = sb(es, "A2", [128, L, NS, 8], F32)
    permF = sb(es, "permF", [128, 128], F32)
    onesD = sb(es, "onesD", [128, 128], BF16)
    blk64 = sb(es, "blk64", [128, 128], BF16)
    scT = sb(es, "scT", [128, 8, NS], F32)
    invf = sb(es, "invf", [128, 1], F32)
    epsc = sb(es, "epsc", [128, 1], F32)
    cst = Res()
    s_c = newsem("const")

    dma(SP, s_c, vecs[:], vecs_d[:, :], writes=[cst])
    dma(SP, s_c, permF[:], perm_d[:, :], writes=[cst])
    op(DVE, lambda: nc.vector.memset(onesD[:], 1.0 / 1024.0), writes=[cst])
    op(DVE, lambda: nc.vector.memset(epsc[:], EPS), writes=[cst])
    op(DVE, lambda: nc.vector.memset(blk64[:], 0.0), writes=[cst])
    op(DVE, lambda: nc.vector.memset(blk64[0:64, 0:64], 1.0 / 64.0), writes=[cst])
    op(DVE, lambda: nc.vector.memset(blk64[64:128, 64:128], 1.0 / 64.0), writes=[cst])
    op(ACT, lambda: nc.scalar.activation(out=scT[:].rearrange("p a b -> p (a b)"),
                                         in_=vecs[:, V_CT:V_CT + 8 * NS], func=AF.Silu), reads=[cst], writes=[cst])
    op(ACT, lambda: nc.scalar.activation(out=invf[:], in_=vecs[:, V_JF:V_JF + 1], func=AF.Exp,
                                         scale=-math.log(10000.0) / 16.0), reads=[cst], writes=[cst])

    with ExitStack() as st0:
        CB = 768
        wst = [sb(st0, "wst%d" % i, [128, 8, CB], F32) for i in range(2)]
        wstR = [Res() for _ in range(2)]
        wstS = [newsem("wst%d" % i) for i in range(2)]
        modR = Res()

        def mod_gen():
          nonlocal_it = [0]
          for l in range(L):
            for cb in range(6 * D // CB):
                it = nonlocal_it[0]
                sl = it % 2
                src = wmod_d[l].rearrange("(kc p) n -> p kc n", p=128)[:, :, cb * CB:(cb + 1) * CB]
                for half in range(2):
                    dma(SP, wstS[sl], wst[sl][:, half * 4:(half + 1) * 4, :], src[:, half * 4:(half + 1) * 4, :],
                        writes=[wstR[sl]])
                for m in range(CB // 128):
                    ch = cb * (CB // 128) + m
                    b = ch % 2

                    def mm(b=b, sl=sl, m=m):
                        for kc in range(8):
                            ins = nc.tensor.matmul(ps[b][:, 0:NS], lhsT=wst[sl][:, kc, m * 128:(m + 1) * 128],
                                                   rhs=scT[:, kc, :], start=(kc == 0), stop=(kc == 7))
                        return ins
                    op(PE, mm, reads=[wstR[sl], cst], writes=[psR[b]])
                    op(DVE, lambda b=b, l=l, ch=ch: nc.vector.tensor_scalar(
                        out=modT[:, l, ch, :], in0=ps[b][:, 0:NS],
                        scalar1=vecs[:, V_BMOD + l * 48 + ch:V_BMOD + l * 48 + ch + 1], scalar2=None, op0=ALU.add),
                        reads=[psR[b], cst], writes=[modR])
                    yield
                nonlocal_it[0] += 1
          return

        def derive_mod():
          for l in range(L):
            for s in range(NS):
                op(DVE, lambda l=l, s=s: nc.vector.scalar_tensor_tensor(
                    out=A1[:, l, s, :], in0=modT[:, l, 8:16, s], scalar=1.0,
                    in1=vecs[:, V_GMIX + l * 8:V_GMIX + l * 8 + 8], op0=ALU.add, op1=ALU.mult),
                    reads=[cst, modR], writes=[cst])
                op(DVE, lambda l=l, s=s: nc.vector.scalar_tensor_tensor(
                    out=A2[:, l, s, :], in0=modT[:, l, 32:40, s], scalar=1.0,
                    in1=vecs[:, V_GFFN + l * 8:V_GFFN + l * 8 + 8], op0=ALU.add, op1=ALU.mult),
                    reads=[cst, modR], writes=[cst])

        RC = min(2048, SMAX)
        posb = sb(st0, "posb", [128, 2, RC], F32)
        ang = sb(st0, "ang", [128, RC], F32)
        tm = sb(st0, "tm", [128, RC], F32)
        tco = sb(st0, "tco", [128, RC], F32)
        tsi = sb(st0, "tsi", [128, RC], F32)
        tmi = sb(st0, "tmi", [128, RC], mybir.dt.int32)
        tmiR = Res()
        posR, angR, tmR, tcoR, tsiR = Res(), Res(), Res(), Res(), Res()
        s_pos, s_tco, s_tsi = newsem("pos"), newsem("tco"), newsem("tsi")
        rotR = Res()
        def rot_gen():
            for c in range(SMAX // RC):
                dma(SP, s_pos, posb[:], pos_d[:, :, c * RC:(c + 1) * RC], writes=[posR])
                op(DVE, lambda: nc.vector.tensor_tensor(out=ang[:], in0=posb[:, 1, :], in1=posb[:, 0, :], op=ALU.subtract),
                   reads=[posR], writes=[angR])
                op(DVE, lambda: nc.vector.scalar_tensor_tensor(out=ang[:], in0=ang[:], scalar=vecs[:, V_AXIS:V_AXIS + 1],
                                                               in1=posb[:, 0, :], op0=ALU.mult, op1=ALU.add),
                   reads=[posR, angR, cst], writes=[angR])
                op(DVE, lambda: nc.vector.tensor_scalar(out=ang[:], in0=ang[:], scalar1=invf[:, 0:1], scalar2=None,
                                                        op0=ALU.mult), reads=[angR, cst], writes=[angR])
                for (phase, dst, dstR) in ((0.0, tsi, tsiR), (0.5 * math.pi, tco, tcoR)):
                    op(DVE, lambda phase=phase: nc.vector.tensor_scalar(
                        out=tm[:], in0=ang[:], scalar1=phase, scalar2=1.0 / (2 * math.pi), op0=ALU.add, op1=ALU.mult),
                        reads=[angR], writes=[tmR])
                    op(DVE, lambda: nc.vector.tensor_copy(out=tmi[:], in_=tm[:]), reads=[tmR], writes=[tmiR])
                    op(DVE, lambda: nc.vector.tensor_copy(out=tm[:], in_=tmi[:]), reads=[tmiR], writes=[tmR])
                    yield
                    op(DVE, lambda: nc.vector.scalar_tensor_tensor(out=tm[:], in0=tm[:], scalar=-2 * math.pi, in1=ang[:],
                                                                   op0=ALU.mult, op1=ALU.add),
                       reads=[tmR, angR], writes=[tmR])
                    op(DVE, lambda phase=phase: nc.vector.tensor_scalar(
                        out=tm[:], in0=tm[:], scalar1=phase, scalar2=-math.pi, op0=ALU.add, op1=ALU.max),
                        reads=[tmR], writes=[tmR])
                    op(DVE, lambda: nc.vector.tensor_scalar(out=tm[:], in0=tm[:], scalar1=math.pi, scalar2=None,
                                                            op0=ALU.min), reads=[tmR], writes=[tmR])
                    op(ACT, lambda dst=dst: nc.scalar.activation(out=dst[:], in_=tm[:], func=AF.Sin),
                       reads=[tmR], writes=[dstR])
                    yield
                op(DVE, lambda: nc.vector.tensor_scalar(out=tsi[:], in0=tsi[:], scalar1=vecs[:, V_SGN:V_SGN + 1],
                                                        scalar2=None, op0=ALU.mult), reads=[tsiR, cst], writes=[tsiR])
                dma(SP, s_tco, cos_d[:, c * RC:(c + 1) * RC], tco[:], reads=[tcoR], writes=[rotR])
                dma(SP, s_tsi, sin_d[:, c * RC:(c + 1) * RC], tsi[:], reads=[tsiR], writes=[rotR])

        interleave(mod_gen(), rot_gen(), 1)
        derive_mod()
        barrier()

    cvt_flip = [0]

    def convert(out_ap, in_ap, reads, writes):
        cvt_flip[0] ^= 1
        if cvt_flip[0]:
            return op(DVE, lambda: nc.vector.tensor_copy(out=out_ap, in_=in_ap), reads=reads, writes=writes)
        return op(ACT, lambda: nc.scalar.copy(out=out_ap, in_=in_ap), reads=reads, writes=writes)

    def seq_of_tok(t):
        for s in range(NS):
            if offs[s] <= t < offs[s] + seqs[s]:
                return s
        raise ValueError

    def rsqrt_act(out_ap, in_ap, reads, writes):
        op(ACT, lambda: nc.scalar.activation(out=out_ap, in_=in_ap, func=AF.Ln, bias=epsc[:, 0:1], scale=1.0),
           reads=reads + [cst], writes=writes)
        op(ACT, lambda: nc.scalar.activation(out=out_ap, in_=out_ap, func=AF.Exp, scale=-0.5),
           reads=writes, writes=writes)

    def norm_h(T, xt, xR, sq, sqR, rstd, rstdR, tmp, tmpR, hT, hR, Avec, shift_col, statb):
        for kc in range(8):
            op(ACT, lambda kc=kc: nc.scalar.activation(out=sq[:, kc, 0:T], in_=xt[:, kc, 0:T], func=AF.Square),
               reads=[xR], writes=[sqR])
            yield

        def mm():
            for kc in range(8):
                ins = nc.tensor.matmul(ps[statb][:, 0:T], lhsT=onesD[:], rhs=sq[:, kc, 0:T],
                                       start=(kc == 0), stop=(kc == 7))
            return ins
        op(PE, mm, reads=[sqR, cst], writes=[psR[statb]])
        yield
        rsqrt_act(rstd[:, 0:T], ps[statb][:, 0:T], [psR[statb]], [rstdR])
        yield
        for kc in range(8):
            k2 = kc % 2
            op(DVE, lambda kc=kc, k2=k2: nc.vector.scalar_tensor_tensor(
                out=tmp[k2][:, 0:T], in0=xt[:, kc, 0:T], scalar=Avec(kc), in1=rstd[:, 0:T],
                op0=ALU.mult, op1=ALU.mult), reads=[xR, rstdR, cst], writes=[tmpR[k2]])
            op(ACT, lambda kc=kc, k2=k2: nc.scalar.activation(
                out=hT[:, kc, 0:T], in_=tmp[k2][:, 0:T], func=AF.Identity, bias=shift_col(kc), scale=1.0),
                reads=[tmpR[k2], cst], writes=[hR])
            yield

    for l in range(L):
        x_in = xT_d if l == 0 else xB_d
        x_out = yT_d if l == L - 1 else xB_d
        last = (l == L - 1)
        with ExitStack() as stL:
            NBMAX = SMAX // 128
            KT = sb(stL, "KT", [128, SMAX], BF16)
            Vaug = sb(stL, "Vaug", [128, NBMAX, 2, 128], BF16)
            KTR, VR = Res(), Res()
            op(DVE, lambda: nc.vector.memset(Vaug[:, :, :, 64:128], 1.0), writes=[VR])
            w_in = sb(stL, "w_in", [128, 8, NIN], BF16)
            woA = sb(stL, "woA", [64, 8, D], BF16)
            woC = sb(stL, "woC", [128, 4, D], BF16)
            winR, woR = Res(), Res()
            with ExitStack() as stg:
                stage = [sb(stg, "stg%d" % i, [128, NIN], F32) for i in range(2)]
                stR = [Res(), Res()]
                stS = [newsem("stg%d" % i) for i in range(2)]
                for kc in range(8):
                    sl = kc % 2
                    dma(SP, stS[sl], stage[sl][:], win_d[l, kc * 128:(kc + 1) * 128, :], writes=[stR[sl]])
                    convert(w_in[:, kc, 0:512].rearrange("p (m g d) -> p m g d", m=4, g=2),
                            stage[sl][:, 0:512].rearrange("p (g m d) -> p m g d", g=2, m=4), [stR[sl]], [winR])
                    convert(w_in[:, kc, 512:NIN], stage[sl][:, 512:NIN], [stR[sl]], [winR])
                for j in range(4):
                    sl = j % 2
                    dma(SP, stS[sl], stage[sl][0:64, 0:2 * D].rearrange("p (a n) -> p a n", a=2),
                        wout_d[l, j * 128:(j + 1) * 128, :].rearrange("(a p) n -> p a n", p=64), writes=[stR[sl]])
                    convert(woA[:, 2 * j:2 * j + 2, :], stage[sl][0:64, 0:2 * D].rearrange("p (a n) -> p a n", a=2),
                            [stR[sl]], [woR])
                for j in range(2):
                    sl = j % 2
                    dma(SP, stS[sl], stage[sl][:, 0:2 * D].rearrange("p (a n) -> p a n", a=2),
                        wout_d[l, 512 + j * 256:512 + (j + 1) * 256, :].rearrange("(a p) n -> p a n", p=128),
                        writes=[stR[sl]])
                    convert(woC[:, 2 * j:2 * j + 2, :], stage[sl][:, 0:2 * D].rearrange("p (a n) -> p a n", a=2),
                            [stR[sl]], [woR])
                barrier()
            for s in range(NS):
                S = seqs[s]
                base = offs[s]
                NB = S // 128
                with ExitStack() as st1:
                    T = 256
                    TH = T + 2
                    xt = [sb(st1, "xt%d" % i, [128, 8, TH], F32) for i in range(2)]
                    hT = [sb(st1, "hT%d" % i, [128, 8, TH], BF16) for i in range(2)]
                    sq = sb(st1, "sq", [128, 8, TH], BF16)
                    rstd = sb(st1, "rstd", [128, TH], F32)
                    tmp = [sb(st1, "tmp%d" % i, [128, TH], F32) for i in range(2)]
                    cosb = [sb(st1, "cosb%d" % i, [128, T], F32) for i in range(3)]
                    sinb = [sb(st1, "sinb%d" % i, [128, T], F32) for i in range(3)]
                    qf = [sb(st1, "qf%d" % i, [128, T], F32) for i in range(2)]
                    sq2 = [sb(st1, "sq2%d" % i, [128, T], BF16) for i in range(2)]
                    rh = [sb(st1, "rh%d" % i, [128, T], F32) for i in range(2)]
                    ta = [sb(st1, "ta%d" % i, [128, T], F32) for i in range(2)]
                    tb_ = [sb(st1, "tb%d" % i, [128, T], F32) for i in range(2)]
                    qTt = [sb(st1, "qTt%d" % i, [128, 4, T], BF16) for i in range(2)]
                    Bsb = sb(st1, "Bsb", [128, 4, T], F32)
                    Csb = sb(st1, "Csb", [128, 4, TH], F32)
                    zb = sb(st1, "zb", [128, 4, TH], F32)
                    yb = [sb(st1, "yb%d" % i, [128, T], F32) for i in range(2)]
                    y0b = sb(st1, "y0b", [128, T], F32)
                    y0R = Res()
                    cvT = [sb(st1, "cvT%d" % i, [128, 4, T], BF16) for i in range(2)]
                    xR = [Res(), Res()]
                    hR = [Res(), Res()]
                    sqR, rstdR = Res(), Res()
                    tmpR = [Res(), Res()]
                    tabR = [Res(), Res(), Res()]
                    qfR, sq2R, rhR, taR, tbR = ([Res(), Res()] for _ in range(5))
                    qTR = [Res(), Res()]
                    BR = [Res() for _ in range(4)]
                    CR = [Res() for _ in range(4)]
                    zR = [Res() for _ in range(4)]
                    ybR = [Res(), Res()]
                    cvR = [Res(), Res()]
                    xS = [newsem("x1_%d" % i) for i in range(2)]
                    tS = [newsem("t1_%d" % i) for i in range(3)]
                    qS_ = [newsem("q1_%d" % i) for i in range(2)]
                    cS_ = [newsem("c1_%d" % i) for i in range(2)]
                    scrR = Res()
                    for i in range(2):
                        op(DVE, lambda i=i: nc.vector.memset(xt[i][:], 0.0), writes=[xR[i]])
                    NT = S // T
                    xsrc = x_in.rearrange("(kc p) t -> p kc t", p=128)
                    pp = [0]
                    qk = [0]
                    wc = V_CONV + l * 12

                    def load1(i):
                        if i >= NT:
                            return
                        sl = i % 2
                        s3 = i % 3
                        t0 = i * T
                        lo = max(t0 - 1, 0)
                        hi = min(t0 + T + 1, S)
                        c0 = lo - (t0 - 1)
                        dma(SP, xS[sl], xt[sl][:, :, c0:c0 + (hi - lo)], xsrc[:, :, base + lo:base + hi],
                            writes=[xR[sl]])
                        dma(SP, tS[s3], cosb[s3][:], cos_d[:, t0:t0 + T], reads=[rotR], writes=[tabR[s3]])
                        dma(SP, tS[s3], sinb[s3][:], sin_d[:, t0:t0 + T], reads=[rotR], writes=[tabR[s3]])

                    def prep_gen(i):
                        sl = i % 2
                        yield from norm_h(TH, xt[sl], xR[sl], sq, sqR, rstd, rstdR, tmp, tmpR, hT[sl], hR[sl],
                                          lambda kc: A1[:, l, s, kc:kc + 1], lambda kc: modT[:, l, kc, s:s + 1], 4)

                    def proj_gen(i):
                        sl = i % 2
                        s3 = i % 3
                        t0 = i * T
                        deferred = []
                        load1(i + 2)

                        def main_mm(m, b):
                            full = m >= 10
                            c_lo, c_n = (0, TH) if full else (1, T)

                            def mmp():
                                for kc in range(8):
                                    ins = nc.tensor.matmul(ps[b][:, 0:c_n], lhsT=w_in[:, kc, m * 128:(m + 1) * 128],
                                                           rhs=hT[sl][:, kc, c_lo:c_lo + c_n],
                                                           start=(kc == 0), stop=(kc == 7))
                                return ins
                            op(PE, mmp, reads=[hR[sl], winR], writes=[psR[b]])

                        def qk_post(m, r):
                            gcol = (V_QG if m < 4 else V_KG) + l
                            gscol = (V_QGS if m < 4 else V_KGS) + l
                            op(PE, lambda: nc.tensor.matmul(ps[5][:, 0:T], lhsT=blk64[:], rhs=sq2[r][:],
                                                            start=True, stop=True),
                               reads=[sq2R[r], cst], writes=[psR[5]])
                            op(PE, lambda: nc.tensor.matmul(ps[6][:, 0:T], lhsT=permF[:], rhs=qf[r][:],
                                                            start=True, stop=True),
                               reads=[qfR[r], cst], writes=[psR[6]])
                            rsqrt_act(rh[r][:], ps[5][:, 0:T], [psR[5]], [rhR[r]])
                            op(DVE, lambda: nc.vector.scalar_tensor_tensor(
                                out=ta[r][:], in0=qf[r][:], scalar=vecs[:, gcol:gcol + 1], in1=cosb[s3][:],
                                op0=ALU.mult, op1=ALU.mult), reads=[qfR[r], tabR[s3], cst], writes=[taR[r]])
                            op(DVE, lambda: nc.vector.scalar_tensor_tensor(
                                out=tb_[r][:], in0=ps[6][:, 0:T], scalar=vecs[:, gscol:gscol + 1], in1=sinb[s3][:],
                                op0=ALU.mult, op1=ALU.mult), reads=[psR[6], tabR[s3], cst], writes=[tbR[r]])
                            op(POOL, lambda: nc.gpsimd.tensor_tensor(out=ta[r][:], in0=ta[r][:], in1=tb_[r][:],
                                                                     op=ALU.add),
                               reads=[taR[r], tbR[r]], writes=[taR[r]])
                            if m < 4:
                                op(DVE, lambda: nc.vector.tensor_tensor(
                                    out=qTt[sl][:, m, :], in0=ta[r][:], in1=rh[r][:], op=ALU.mult),
                                    reads=[taR[r], rhR[r]], writes=[qTR[sl]])
                            else:
                                op(DVE, lambda: nc.vector.tensor_tensor(
                                    out=KT[:, t0:t0 + T], in0=ta[r][:], in1=rh[r][:], op=ALU.mult),
                                    reads=[taR[r], rhR[r]], writes=[KTR])

                        order = [0, 1, 2, 3, 4] + list(range(10, 14)) + list(range(6, 10)) + list(range(14, 18))
                        for n, m in enumerate(order):
                            b = pp[0] % 4
                            pp[0] += 1
                            main_mm(m, b)
                            if len(deferred) >= 2:
                                deferred.pop(0)()
                            if m < 5:
                                r = qk[0] % 2
                                qk[0] += 1
                                op(ACT, lambda: nc.scalar.copy(out=qf[r][:], in_=ps[b][:, 0:T]),
                                   reads=[psR[b]], writes=[qfR[r]])
                                op(ACT, lambda: nc.scalar.activation(out=sq2[r][:], in_=ps[b][:, 0:T],
                                                                     func=AF.Square),
                                   reads=[psR[b]], writes=[sq2R[r]])
                                deferred.append(lambda m=m, r=r: qk_post(m, r))
                            elif m < 10:
                                cc = m - 6
                                op(ACT, lambda: nc.scalar.copy(out=Bsb[:, cc, :], in_=ps[b][:, 0:T]),
                                   reads=[psR[b]], writes=[BR[cc]])
                            elif m < 14:
                                cc = m - 10
                                op(ACT, lambda: nc.scalar.copy(out=Csb[:, cc, :], in_=ps[b][:, 0:TH]),
                                   reads=[psR[b]], writes=[CR[cc]])
                            else:
                                cc = m - 14
                                op(DVE, lambda: nc.vector.tensor_tensor(
                                    out=zb[:, cc, :], in0=Csb[:, cc, :], in1=ps[b][:, 0:TH], op=ALU.mult),
                                    reads=[psR[b], CR[cc]], writes=[zR[cc]])
                                if i == 0:
                                    op(DVE, lambda: nc.vector.memset(zb[:, cc, 0:1], 0.0), writes=[zR[cc]])
                                if i == NT - 1:
                                    op(DVE, lambda: nc.vector.memset(zb[:, cc, TH - 1:TH], 0.0), writes=[zR[cc]])
                                y2 = cc % 2
                                op(DVE, lambda: nc.vector.scalar_tensor_tensor(
                                    out=yb[y2][:], in0=ps[b][:, 1:1 + T], scalar=vecs[:, wc + 4 + cc:wc + 5 + cc],
                                    in1=Csb[:, cc, 1:1 + T], op0=ALU.mult, op1=ALU.mult),
                                    reads=[psR[b], CR[cc], cst], writes=[ybR[y2]])

                                def conv_rest(cc=cc, y2=y2):
                                    for (zo, wo) in ((0, 0), (2, 8)):
                                        op(DVE, lambda: nc.vector.scalar_tensor_tensor(
                                            out=yb[y2][:], in0=zb[:, cc, zo:zo + T],
                                            scalar=vecs[:, wc + wo + cc:wc + wo + cc + 1], in1=yb[y2][:],
                                            op0=ALU.mult, op1=ALU.add), reads=[zR[cc], ybR[y2], cst],
                                            writes=[ybR[y2]])
                                    op(POOL, lambda: nc.gpsimd.tensor_tensor(
                                        out=cvT[sl][:, cc, :], in0=yb[y2][:], in1=Bsb[:, cc, :], op=ALU.mult),
                                        reads=[ybR[y2], BR[cc]], writes=[cvR[sl]])
                                deferred.append(conv_rest)
                            yield
                        while deferred:
                            deferred.pop(0)()
                        for tb in range(T // 128):
                            def mmv(tb=tb):
                                for kc in range(8):
                                    ins = nc.tensor.matmul(
                                        ps[7][:, tb * 128:(tb + 1) * 128],
                                        lhsT=hT[sl][:, kc, 1 + tb * 128:1 + (tb + 1) * 128],
                                        rhs=w_in[:, kc, 640:768], start=(kc == 0), stop=(kc == 7))
                                return ins
                            op(PE, mmv, reads=[hR[sl], winR], writes=[psR[7]])
                        for tb in range(T // 128):
                            blk = (t0 // 128) + tb
                            op(ACT, lambda tb=tb, blk=blk: nc.scalar.copy(
                                out=Vaug[:, blk, :, 0:64],
                                in_=ps[7][:, tb * 128:(tb + 1) * 128].rearrange("p (g d) -> p g d", g=2)),
                                reads=[psR[7]], writes=[VR])
                        yield
                        dma(SP, qS_[sl],
                            qS_d.rearrange("h d t -> (h d) t").rearrange("(c p) t -> p c t", p=128)[
                                :, :, base + t0:base + t0 + T],
                            qTt[sl][:], reads=[qTR[sl]], writes=[scrR])
                        dma(SP, cS_[sl], cS_d.rearrange("(c p) t -> p c t", p=128)[:, :, base + t0:base + t0 + T],
                            cvT[sl][:], reads=[cvR[sl]], writes=[scrR])

                    load1(0)
                    load1(1)
                    run_all(prep_gen(0))
                    for i in range(NT):
                        if i + 1 < NT:
                            interleave(proj_gen(i), prep_gen(i + 1), 2)
                        else:
                            run_all(proj_gen(i))
                    barrier()
                with ExitStack() as st2:
                    T = 512
                    q2 = [sb(st2, "q2_%d" % i, [128, 8, T], BF16) for i in range(2)]
                    c2 = [sb(st2, "c2_%d" % i, [128, 4, T], BF16) for i in range(2)]
                    x2 = [sb(st2, "x2_%d" % i, [128, 8, T], F32) for i in range(2)]
                    Pb = [sb(st2, "Pb%d" % i, [128, 2 * T], BF16) for i in range(3)]
                    rcb = [sb(st2, "rcb%d" % i, [128, T], F32) for i in range(2)]
                    attn = [sb(st2, "attn%d" % i, [64, 8, T], BF16) for i in range(2)]
                    q2R, c2R, x2R = [Res(), Res()], [Res(), Res()], [Res(), Res()]
                    PbR = [Res() for _ in range(3)]
                    rcR = [Res(), Res()]
                    attnR = [Res(), Res()]
                    q2S = [newsem("q2_%d" % i) for i in range(2)]
                    c2S = [newsem("c2_%d" % i) for i in range(2)]
                    x2S = [newsem("x2_%d" % i) for i in range(2)]
                    x1R = Res()
                    NT = S // T
                    xsrc = x_in.rearrange("(kc p) t -> p kc t", p=128)
                    xdst = xA_d.rearrange("(kc p) t -> p kc t", p=128)
                    qsrc = qS_d.rearrange("h d t -> (h d) t").rearrange("(m p) t -> p m t", p=128)
                    for i in range(2):
                        op(DVE, lambda i=i: nc.vector.memset(q2[i][:], 0.0), writes=[q2R[i]])
                    csrc = cS_d.rearrange("(c p) t -> p c t", p=128)
                    xo = [0]
                    NBP = NB // 2

                    def load2(i):
                        if i >= NT:
                            return
                        sl = i % 2
                        t0 = base + i * T
                        dma(SP, q2S[sl], q2[sl][0:64, 0:4, :], qsrc[0:64, :, t0:t0 + T], reads=[scrR],
                            writes=[q2R[sl]])
                        dma(SP, q2S[sl], q2[sl][64:128, 4:8, :], qsrc[64:128, :, t0:t0 + T], reads=[scrR],
                            writes=[q2R[sl]])
                        dma(SP, c2S[sl], c2[sl][:], csrc[:, :, t0:t0 + T], reads=[scrR], writes=[c2R[sl]])
                        dma(SP, x2S[sl], x2[sl][:], xsrc[:, :, t0:t0 + T], writes=[x2R[sl]])

                    def attn_gen(i):
                        sl = i % 2
                        t0 = base + i * T
                        groups = [(h, bp) for h in range(8) for bp in range(NBP)]
                        NG = len(groups)

                        def QK2(g):
                            h, bp = groups[g]
                            kv = h // 4
                            b = g % 2

                            def f():
                                for j in range(2):
                                    blk = 2 * bp + j
                                    ins = nc.tensor.matmul(psP[b][:, j * 512:(j + 1) * 512],
                                                           lhsT=KT[:, blk * 128:(blk + 1) * 128],
                                                           rhs=q2[sl][:, h, :], start=True, stop=True)
                                return ins
                            op(PE, f, reads=[KTR, q2R[sl]], writes=[psR[2 * b], psR[2 * b + 1]])

                        def EXP2(g):
                            b = g % 2
                            p3 = g % 3
                            op(ACT, lambda: nc.scalar.activation(out=Pb[p3][:], in_=psP[b][:, :], func=AF.Exp,
                                                                 scale=0.125),
                               reads=[psR[2 * b], psR[2 * b + 1]], writes=[PbR[p3]])

                        def PV2(g):
                            h, bp = groups[g]
                            kv = h // 4
                            p3 = g % 3
                            o = 4 + (h % 2)

                            def f():
                                for j in range(2):
                                    blk = 2 * bp + j
                                    ins = nc.tensor.matmul(ps[o][:, 0:T], lhsT=Vaug[:, blk, kv, :],
                                                           rhs=Pb[p3][:, j * 512:(j + 1) * 512],
                                                           start=(blk == 0), stop=(blk == NB - 1))
                                return ins
                            op(PE, f, reads=[VR, PbR[p3]], writes=[psR[o]])
                            if bp == NBP - 1:
                                r = h % 2
                                op(DVE, lambda: nc.vector.reciprocal(out=rcb[r][64:128, :], in_=ps[o][64:128, 0:T]),
                                   reads=[psR[o]], writes=[rcR[r]])
                                op(DVE, lambda: nc.vector.tensor_tensor(out=attn[sl][0:64, h, :], in0=ps[o][0:64, 0:T],
                                                                        in1=rcb[r][64:128, :], op=ALU.mult),
                                   reads=[psR[o], rcR[r]], writes=[attnR[sl]])
                        QK2(0)
                        if NG > 1:
                            QK2(1)
                        for g in range(NG):
                            EXP2(g)
                            if g + 2 < NG:
                                QK2(g + 2)
                            PV2(g)
                            if g == NG // 2:
                                load2(i + 1)
                            yield

                    def oproj_gen(i):
                        sl = i % 2
                        t0 = base + i * T
                        for m in range(8):
                            b = 6 + (xo[0] % 2)
                            xo[0] += 1

                            def mmo(m=m, b=b):
                                for h in range(8):
                                    nc.tensor.matmul(ps[b][:, 0:T], lhsT=woA[0:64, h, m * 128:(m + 1) * 128],
                                                     rhs=attn[sl][0:64, h, :], start=(h == 0), stop=False)
                                for cc in range(4):
                                    ins = nc.tensor.matmul(ps[b][:, 0:T], lhsT=woC[:, cc, m * 128:(m + 1) * 128],
                                                           rhs=c2[sl][:, cc, :], start=False, stop=(cc == 3))
                                return ins
                            op(PE, mmo, reads=[attnR[sl], c2R[sl], woR], writes=[psR[b]])
                            op(DVE, lambda m=m, b=b: nc.vector.scalar_tensor_tensor(
                                out=x2[sl][:, m, :], in0=ps[b][:, 0:T], scalar=modT[:, l, 16 + m, s:s + 1],
                                in1=x2[sl][:, m, :], op0=ALU.mult, op1=ALU.add),
                                reads=[psR[b], x2R[sl], cst], writes=[x2R[sl]])
                            yield
                        dma(SP, x2S[sl], xdst[:, :, t0:t0 + T], x2[sl][:], reads=[x2R[sl]], writes=[x1R])

                    pending = None
                    load2(0)
                    for i in range(NT):
                        cnt = 0
                        for _ in attn_gen(i):
                            cnt += 1
                            if pending is not None and cnt % 4 == 0:
                                if next(pending, "end") == "end":
                                    pending = None
                        if pending is not None:
                            for _ in pending:
                                pass
                        pending = oproj_gen(i)
                    for _ in pending:
                        pass
                    barrier()
            barrier()
        with ExitStack() as st3:
            T = 256
            wg = sb(st3, "wg", [128, 8, DFF], BF16)
            wu = sb(st3, "wu", [128, 8, DFF], BF16)
            wd = sb(st3, "wd", [128, FC, D], BF16)
            wgR, wuR, wdR = Res(), Res(), Res()
            HW = DFF // 2
            stage = [sb(st3, "stgf%d" % i, [128, HW], F32) for i in range(3)]
            stR = [Res(), Res(), Res()]
            stS = [newsem("stgf%d" % i) for i in range(3)]

            def wload_gen():
                j = 0
                for (wsrc, wdst, wr) in ((wg_d, wg, wgR), (wu_d, wu, wuR)):
                    for kc in range(8):
                        for hf in range(2):
                            sl = j % 3
                            j += 1
                            dma(SP, stS[sl], stage[sl][:], wsrc[l, kc * 128:(kc + 1) * 128, hf * HW:(hf + 1) * HW],
                                writes=[stR[sl]])
                            convert(wdst[:, kc, hf * HW:(hf + 1) * HW], stage[sl][:], [stR[sl]], [wr])
                        yield
                for jj in range(FC):
                    sl = j % 3
                    j += 1
                    dma(SP, stS[sl], stage[sl][:, 0:D], wd_d[l, jj * 128:(jj + 1) * 128, :], writes=[stR[sl]])
                    convert(wd[:, jj, :], stage[sl][:, 0:D], [stR[sl]], [wdR])
                    if jj % 2 == 1:
                        yield
            x3 = [sb(st3, "x3_%d" % i, [128, 8, T], F32) for i in range(3)]
            h3 = [sb(st3, "h3_%d" % i, [128, 8, T], BF16) for i in range(2)]
            sq = sb(st3, "sq3", [128, 8, T], BF16)
            rstd = sb(st3, "rstd3", [128, T], F32)
            tmp = [sb(st3, "tmp3_%d" % i, [128, T], F32) for i in range(2)]
            sg = [sb(st3, "sg%d" % i, [128, T], F32) for i in range(2)]
            aT = sb(st3, "aT", [128, FC, T], BF16)
            x3R, h3R = [Res(), Res(), Res()], [Res(), Res()]
            sqR, rstdR = Res(), Res()
            tmpR = [Res(), Res()]
            sgR = [Res(), Res()]
            aR = Res()
            x3S = [newsem("x3_%d" % i) for i in range(3)]
            outR = Res()
            xsrc = xA_d.rearrange("(kc p) t -> p kc t", p=128)
            xdst = x_out.rearrange("(kc p) t -> p kc t", p=128)
            NT = NTOK // T
            sqf, rstdf = sq, rstd
            sqfR, rstdfR = sqR, rstdR
            dn = [0]

            def prep3(i):
                sl = i % 2
                xs = i % 3
                t0 = i * T
                s = seq_of_tok(t0)
                dma(SP, x3S[xs], x3[xs][:], xsrc[:, :, t0:t0 + T], writes=[x3R[xs]])
                yield
                yield from norm_h(T, x3[xs], x3R[xs], sq, sqR, rstd, rstdR, tmp, tmpR, h3[sl], h3R[sl],
                                  lambda kc: A2[:, l, s, kc:kc + 1], lambda kc: modT[:, l, 24 + kc, s:s + 1], 4)

            def ffn3(i):
                sl = i % 2
                xs = i % 3
                t0 = i * T
                s = seq_of_tok(t0)
                for j in range(FC):
                    bg = j % 2
                    bu = 2 + (j % 2)

                    def mmg():
                        for kc in range(8):
                            ins = nc.tensor.matmul(ps[bg][:, 0:T], lhsT=wg[:, kc, j * 128:(j + 1) * 128],
                                                   rhs=h3[sl][:, kc, :], start=(kc == 0), stop=(kc == 7))
                        return ins

                    def mmu():
                        for kc in range(8):
                            ins = nc.tensor.matmul(ps[bu][:, 0:T], lhsT=wu[:, kc, j * 128:(j + 1) * 128],
                                                   rhs=h3[sl][:, kc, :], start=(kc == 0), stop=(kc == 7))
                        return ins
                    op(PE, mmg, reads=[h3R[sl], wgR], writes=[psR[bg]])
                    op(PE, mmu, reads=[h3R[sl], wuR], writes=[psR[bu]])
                    op(ACT, lambda: nc.scalar.activation(out=sg[bg][:], in_=ps[bg][:, 0:T], func=AF.Silu),
                       reads=[psR[bg]], writes=[sgR[bg]])
                    op(DVE, lambda: nc.vector.tensor_tensor(
                        out=aT[:, j, :], in0=sg[bg][:], in1=ps[bu][:, 0:T], op=ALU.mult),
                        reads=[sgR[bg], psR[bu]], writes=[aR])
                    if j % 2 == 1:
                        yield
                for m in range(8):
                    b = 5 + (dn[0] % 2)
                    dn[0] += 1

                    def mmd():
                        for j in range(FC):
                            ins = nc.tensor.matmul(ps[b][:, 0:T], lhsT=wd[:, j, m * 128:(m + 1) * 128],
                                                   rhs=aT[:, j, :], start=(j == 0), stop=(j == FC - 1))
                        return ins
                    op(PE, mmd, reads=[aR, wdR], writes=[psR[b]])
                    op(DVE, lambda: nc.vector.scalar_tensor_tensor(
                        out=x3[xs][:, m, :], in0=ps[b][:, 0:T], scalar=modT[:, l, 40 + m, s:s + 1],
                        in1=x3[xs][:, m, :], op0=ALU.mult, op1=ALU.add),
                        reads=[psR[b], x3R[xs], cst], writes=[x3R[xs]])
                    yield

            def fin3(i):
                sl = i % 2
                xs = i % 3
                t0 = i * T
                if last:
                    for kc in range(8):
                        op(ACT, lambda kc=kc: nc.scalar.activation(out=sqf[:, kc, :], in_=x3[xs][:, kc, :],
                                                                   func=AF.Square), reads=[x3R[xs]], writes=[sqfR])
                        if kc % 2 == 1:
                            yield

                    def mmf():
                        for kc in range(8):
                            ins = nc.tensor.matmul(ps[7][:, 0:T], lhsT=onesD[:], rhs=sqf[:, kc, :],
                                                   start=(kc == 0), stop=(kc == 7))
                        return ins
                    op(PE, mmf, reads=[sqfR, cst], writes=[psR[7]])
                    yield
                    rsqrt_act(rstdf[:], ps[7][:, 0:T], [psR[7]], [rstdfR])
                    yield
                    for kc in range(8):
                        op(DVE, lambda kc=kc: nc.vector.scalar_tensor_tensor(
                            out=x3[xs][:, kc, :], in0=x3[xs][:, kc, :], scalar=vecs[:, V_GFIN + kc:V_GFIN + kc + 1],
                            in1=rstdf[:], op0=ALU.mult, op1=ALU.mult), reads=[x3R[xs], rstdfR, cst],
                            writes=[x3R[xs]])
                        if kc % 2 == 1:
                            yield
                dma(SP, x3S[xs], xdst[:, :, t0:t0 + T], x3[xs][:], reads=[x3R[xs]], writes=[outR])

            def chain(*gens):
                for g in gens:
                    yield from g

            p0 = prep3(0)
            next(p0)
            interleave(wload_gen(), p0, 1)
            for i in range(NT):
                side = []
                if i >= 1:
                    side.append(fin3(i - 1))
                if i + 1 < NT:
                    side.append(prep3(i + 1))
                interleave(ffn3(i), chain(*side), 2)
            run_all(fin3(NT - 1))
            barrier()
    barrier()
    es.close()
    return nc


def _pack_vecs(c_list, b_mod, g_mix, g_ffn, g_final, q_gain, k_gain, conv_w):
    NS = len(c_list)
    v = np.zeros((128, NV), np.float32)
    cT = np.stack(c_list, 0).reshape(NS, 8, 128).transpose(2, 1, 0)
    v[:, V_CT:V_CT + 8 * NS] = cT.reshape(128, 8 * NS)
    v[:, V_BMOD:V_BMOD + L * 48] = b_mod.reshape(L, 48, 128).transpose(2, 0, 1).reshape(128, L * 48)
    v[:, V_GMIX:V_GMIX + L * 8] = g_mix.reshape(L, 8, 128).transpose(2, 0, 1).reshape(128, L * 8)
    v[:, V_GFFN:V_GFFN + L * 8] = g_ffn.reshape(L, 8, 128).transpose(2, 0, 1).reshape(128, L * 8)
    v[:, V_GFIN:V_GFIN + 8] = g_final.reshape(8, 128).T
    p = np.arange(128)
    d = p % 64
    for l in range(L):
        v[:, V_QG + l] = q_gain[l][d]
        v[:, V_QGS + l] = q_gain[l][d ^ 1]
        v[:, V_KG + l] = k_gain[l][d]
        v[:, V_KGS + l] = k_gain[l][d ^ 1]
    v[:, V_CONV:V_CONV + L * 12] = conv_w.reshape(L, 3, 4, 128).transpose(3, 0, 1, 2).reshape(128, L * 12)
    pair = d // 2
    v[:, V_JF] = (pair % 16).astype(np.float32)
    v[:, V_AXIS] = (pair // 16).astype(np.float32)
    v[:, V_SGN] = np.where(d % 2 == 0, -1.0, 1.0).astype(np.float32)
    return v


def _consts(SMAX):
    t = np.arange(SMAX)
    pos = np.stack([(t // 64), (t % 64)], 0).astype(np.float32)
    pos = np.ascontiguousarray(np.broadcast_to(pos[None], (128, 2, SMAX)))
    perm = np.zeros((128, 128), np.float32)
    perm[np.arange(128) ^ 1, np.arange(128)] = 1.0
    return pos, perm


def run(seqs, x_list, c_lists, w):
    ncores = len(x_list)
    nc = build(seqs)
    pos, perm = _consts(max(seqs))
    in_maps = []
    for c in range(ncores):
        xT = np.ascontiguousarray(np.concatenate([np.asarray(x).T for x in x_list[c]], axis=1))
        vecs = _pack_vecs(c_lists[c], w["b_mod"], w["g_mix"], w["g_ffn"], w["g_final"], w["q_gain"], w["k_gain"],
                          w["conv_w"])
        in_maps.append({"xT": xT, "vecs": vecs, "pos": pos, "perm": perm, "w_mod": w["w_mod"], "w_in": w["w_in"],
                        "w_out": w["w_out"], "w_gate": w["w_gate"], "w_up": w["w_up"], "w_down": w["w_down"]})
    res = run_bass_kernel_spmd(nc, in_maps, core_ids=list(range(ncores)))
    outs = []
    for c in range(ncores):
        yT = res.results[c]["yT"]
        o = []
        off = 0
        for S in seqs:
            o.append(np.ascontiguousarray(yT[:, off:off + S].T))
            off += S
        outs.append(o)
    return outs


def kernel(x_prompt, x_sample, c_prompt, c_sample, w_mod, b_mod, g_mix, w_in, q_gain, k_gain, conv_w, w_out,
           g_ffn, w_gate, w_up, w_down, g_final):
    f = lambda a: np.ascontiguousarray(np.asarray(a, dtype=np.float32))
    x_prompt, x_sample, c_prompt, c_sample = f(x_prompt), f(x_sample), f(c_prompt), f(c_sample)
    w = {k: f(v) for k, v in dict(w_mod=w_mod, b_mod=b_mod, g_mix=g_mix, w_in=w_in, q_gain=q_gain, k_gain=k_gain,
                                  conv_w=conv_w, w_out=w_out, g_ffn=g_ffn, w_gate=w_gate, w_up=w_up, w_down=w_down,
                                  g_final=g_final).items()}
    seqs = [x_prompt.shape[1], x_sample.shape[1], x_sample.shape[1]]
    x_list = [[x_prompt[c], x_sample[2 * c], x_sample[2 * c + 1]] for c in range(NCORES)]
    c_lists = [[c_prompt[c], c_sample[2 * c], c_sample[2 * c + 1]] for c in range(NCORES)]
    outs = run(seqs, x_list, c_lists, w)
    y_prompt = np.stack([outs[c][0] for c in range(NCORES)], 0)
    y_sample = np.stack([outs[c // 2][1 + (c % 2)] for c in range(2 * NCORES)], 0)
    return (y_prompt, y_sample)
```

```python
import math
from contextlib import ExitStack
import numpy as np
import concourse.bass as bass
import concourse.mybir as mybir
from concourse.bass_utils import run_bass_kernel_spmd

F32 = mybir.dt.float32
BF16 = mybir.dt.bfloat16
AF = mybir.ActivationFunctionType
ALU = mybir.AluOpType

D = 1024
KC = 8
L = 2
DFF = 2816
FC = 22
NIN = 2304
EPS = 1e-6
NCORES = 8
HD = 64

V_CT = 0
V_BMOD = 24
V_GMIX = 120
V_GFFN = 136
V_GFIN = 152
V_QG = 160
V_QGS = 162
V_KG = 164
V_KGS = 166
V_CONV = 168
V_JF = 192
V_AXIS = 193
V_SGN = 194
NV = 196


class Sem:
    def __init__(self, nc, es, name):
        self.h = es.enter_context(nc.semaphore(name))
        self.cnt = 0
        self.name = name


class Eng:
    def __init__(self, nc, es, e, name):
        self.e = e
        self.sem = Sem(nc, es, "s_" + name)
        self.seen = {}

    def wait(self, toks):
        for (sem, v) in toks:
            if self.seen.get(sem, 0) >= v:
                continue
            self.e.wait_ge(sem.h, v)
            self.seen[sem] = v


class Res:
    __slots__ = ("w", "r")

    def __init__(self):
        self.w = None
        self.r = {}


def _deps(eng, reads, writes):
    toks = []
    for r in reads:
        if r.w is not None:
            toks.append(r.w)
    for w in writes:
        if w.w is not None:
            toks.append(w.w)
        for s, v in w.r.items():
            if s is not eng.sem:
                toks.append((s, v))
    eng.wait(toks)


def _commit(tok, reads, writes):
    for r in reads:
        if r.r.get(tok[0], 0) < tok[1]:
            r.r[tok[0]] = tok[1]
    for w in writes:
        w.w = tok
        w.r = {}


def op(eng, fn, reads=(), writes=()):
    _deps(eng, reads, writes)
    ins = fn()
    eng.sem.cnt += 1
    ins.then_inc(eng.sem.h, 1)
    tok = (eng.sem, eng.sem.cnt)
    _commit(tok, reads, writes)
    return tok


def dma(q, sem, out, in_, reads=(), writes=()):
    _deps(q, reads, writes)
    ins = q.e.dma_start(out=out, in_=in_)
    sem.cnt += 16
    ins.then_inc(sem.h, 16)
    tok = (sem, sem.cnt)
    _commit(tok, reads, writes)
    return tok


def run_all(g):
    for _ in g:
        pass

def interleave(ga, gb, nb=1):
    da = db = False
    while not (da and db):
        if not da and next(ga, "end") == "end":
            da = True
        for _ in range(nb):
            if not db and next(gb, "end") == "end":
                db = True


def build(seqs):
    NTOK = sum(seqs)
    SMAX = max(seqs)
    NS = len(seqs)
    offs = [sum(seqs[:i]) for i in range(NS)]
    nc = bass.Bass("TRN2", target_bir_lowering=False)

    def din(name, shape, dt=F32):
        return nc.dram_tensor(name, shape, dt, kind="ExternalInput").ap()

    xT_d = din("xT", [D, NTOK])
    vecs_d = din("vecs", [128, NV])
    pos_d = din("pos", [128, 2, SMAX])
    perm_d = din("perm", [128, 128])
    wmod_d = din("w_mod", [L, D, 6 * D])
    win_d = din("w_in", [L, D, NIN])
    wout_d = din("w_out", [L, D, D])
    wg_d = din("w_gate", [L, D, DFF])
    wu_d = din("w_up", [L, D, DFF])
    wd_d = din("w_down", [L, DFF, D])
    yT_d = nc.dram_tensor("yT", [D, NTOK], F32, kind="ExternalOutput").ap()
    xA_d = nc.dram_tensor("xA", [D, NTOK], F32, kind="Internal").ap()
    xB_d = nc.dram_tensor("xB", [D, NTOK], F32, kind="Internal").ap()
    qS_d = nc.dram_tensor("qS", [8, HD, NTOK], BF16, kind="Internal").ap()
    cS_d = nc.dram_tensor("cS", [512, NTOK], BF16, kind="Internal").ap()
    cos_d = nc.dram_tensor("cosT", [128, SMAX], F32, kind="Internal").ap()
    sin_d = nc.dram_tensor("sinT", [128, SMAX], F32, kind="Internal").ap()

    es = ExitStack()
    PE = Eng(nc, es, nc.tensor, "pe")
    ACT = Eng(nc, es, nc.scalar, "act")
    DVE = Eng(nc, es, nc.vector, "dve")
    POOL = Eng(nc, es, nc.gpsimd, "pool")
    SP = Eng(nc, es, nc.sync, "sp")
    engines = [PE, ACT, DVE, POOL, SP]
    all_sems = [e.sem for e in engines]
    sem_ctr = [0]

    sem_pool = {}

    def newsem(name):
        if name in sem_pool:
            return sem_pool[name]
        sem_ctr[0] += 1
        s = Sem(nc, es, "d%d_%s" % (sem_ctr[0], name))
        all_sems.append(s)
        sem_pool[name] = s
        return s

    def barrier():
        toks = [(s, s.cnt) for s in all_sems if s.cnt > 0]
        for e in engines:
            e.wait(toks)

    sb_ctr = [0]

    def sb(stack, name, shape, dt):
        sb_ctr[0] += 1
        return stack.enter_context(nc.sbuf_tensor("sb%d_%s" % (sb_ctr[0], name), shape, dt))

    psP = [es.enter_context(nc.psum_tensor("psP%d" % i, [128, 1024], F32)) for i in range(4)]
    ps = []
    for i in range(4):
        ps.append(psP[i][:, 0:512])
        ps.append(psP[i][:, 512:1024])
    psR = [Res() for _ in range(8)]

    vecs = sb(es, "vecs", [128, NV], F32)
    modT = sb(es, "modT", [128, L, 48, NS], F32)
    A1 = sb(es, "A1", [128, L, NS, 8], F32)
    A2 = sb(es, "A2", [128, L, NS, 8], F32)
    permF = sb(es, "permF", [128, 128], F32)
    onesD = sb(es, "onesD", [128, 128], BF16)
    blk64 = sb(es, "blk64", [128, 128], BF16)
    scT = sb(es, "scT", [128, 8, NS], F32)
    invf = sb(es, "invf", [128, 1], F32)
    epsc = sb(es, "epsc", [128, 1], F32)
    cst = Res()
    s_c = newsem("const")

    dma(SP, s_c, vecs[:], vecs_d[:, :], writes=[cst])
    dma(SP, s_c, permF[:], perm_d[:, :], writes=[cst])
    op(DVE, lambda: nc.vector.memset(onesD[:], 1.0 / 1024.0), writes=[cst])
    op(DVE, lambda: nc.vector.memset(epsc[:], EPS), writes=[cst])
    op(DVE, lambda: nc.vector.memset(blk64[:], 0.0), writes=[cst])
    op(DVE, lambda: nc.vector.memset(blk64[0:64, 0:64], 1.0 / 64.0), writes=[cst])
    op(DVE, lambda: nc.vector.memset(blk64[64:128, 64:128], 1.0 / 64.0), writes=[cst])
    op(ACT, lambda: nc.scalar.activation(out=scT[:].rearrange("p a b -> p (a b)"),
                                         in_=vecs[:, V_CT:V_CT + 8 * NS], func=AF.Silu), reads=[cst], writes=[cst])
    op(ACT, lambda: nc.scalar.activation(out=invf[:], in_=vecs[:, V_JF:V_JF + 1], func=AF.Exp,
                                         scale=-math.log(10000.0) / 16.0), reads=[cst], writes=[cst])

    with ExitStack() as st0:
        CB = 768
        wst = [sb(st0, "wst%d" % i, [128, 8, CB], F32) for i in range(2)]
        wstR = [Res() for _ in range(2)]
        wstS = [newsem("wst%d" % i) for i in range(2)]
        modR = Res()

        def mod_gen():
          nonlocal_it = [0]
          for l in range(L):
            for cb in range(6 * D // CB):
                it = nonlocal_it[0]
                sl = it % 2
                src = wmod_d[l].rearrange("(kc p) n -> p kc n", p=128)[:, :, cb * CB:(cb + 1) * CB]
                for half in range(2):
                    dma(SP, wstS[sl], wst[sl][:, half * 4:(half + 1) * 4, :], src[:, half * 4:(half + 1) * 4, :],
                        writes=[wstR[sl]])
                for m in range(CB // 128):
                    ch = cb * (CB // 128) + m
                    b = ch % 2

                    def mm(b=b, sl=sl, m=m):
                        for kc in range(8):
                            ins = nc.tensor.matmul(ps[b][:, 0:NS], lhsT=wst[sl][:, kc, m * 128:(m + 1) * 128],
                                                   rhs=scT[:, kc, :], start=(kc == 0), stop=(kc == 7))
                        return ins
                    op(PE, mm, reads=[wstR[sl], cst], writes=[psR[b]])
                    op(DVE, lambda b=b, l=l, ch=ch: nc.vector.tensor_scalar(
                        out=modT[:, l, ch, :], in0=ps[b][:, 0:NS],
                        scalar1=vecs[:, V_BMOD + l * 48 + ch:V_BMOD + l * 48 + ch + 1], scalar2=None, op0=ALU.add),
                        reads=[psR[b], cst], writes=[modR])
                    yield
                nonlocal_it[0] += 1
          return

        def derive_mod():
          for l in range(L):
            for s in range(NS):
                op(DVE, lambda l=l, s=s: nc.vector.scalar_tensor_tensor(
                    out=A1[:, l, s, :], in0=modT[:, l, 8:16, s], scalar=1.0,
                    in1=vecs[:, V_GMIX + l * 8:V_GMIX + l * 8 + 8], op0=ALU.add, op1=ALU.mult),
                    reads=[cst, modR], writes=[cst])
                op(DVE, lambda l=l, s=s: nc.vector.scalar_tensor_tensor(
                    out=A2[:, l, s, :], in0=modT[:, l, 32:40, s], scalar=1.0,
                    in1=vecs[:, V_GFFN + l * 8:V_GFFN + l * 8 + 8], op0=ALU.add, op1=ALU.mult),
                    reads=[cst, modR], writes=[cst])

        RC = min(2048, SMAX)
        posb = sb(st0, "posb", [128, 2, RC], F32)
        ang = sb(st0, "ang", [128, RC], F32)
        tm = sb(st0, "tm", [128, RC], F32)
        tco = sb(st0, "tco", [128, RC], F32)
        tsi = sb(st0, "tsi", [128, RC], F32)
        tmi = sb(st0, "tmi", [128, RC], mybir.dt.int32)
        tmiR = Res()
        posR, angR, tmR, tcoR, tsiR = Res(), Res(), Res(), Res(), Res()
        s_pos, s_tco, s_tsi = newsem("pos"), newsem("tco"), newsem("tsi")
        rotR = Res()
        def rot_gen():
            for c in range(SMAX // RC):
                dma(SP, s_pos, posb[:], pos_d[:, :, c * RC:(c + 1) * RC], writes=[posR])
                op(DVE, lambda: nc.vector.tensor_tensor(out=ang[:], in0=posb[:, 1, :], in1=posb[:, 0, :], op=ALU.subtract),
                   reads=[posR], writes=[angR])
                op(DVE, lambda: nc.vector.scalar_tensor_tensor(out=ang[:], in0=ang[:], scalar=vecs[:, V_AXIS:V_AXIS + 1],
                                                               in1=posb[:, 0, :], op0=ALU.mult, op1=ALU.add),
                   reads=[posR, angR, cst], writes=[angR])
                op(DVE, lambda: nc.vector.tensor_scalar(out=ang[:], in0=ang[:], scalar1=invf[:, 0:1], scalar2=None,
                                                        op0=ALU.mult), reads=[angR, cst], writes=[angR])
                for (phase, dst, dstR) in ((0.0, tsi, tsiR), (0.5 * math.pi, tco, tcoR)):
                    op(DVE, lambda phase=phase: nc.vector.tensor_scalar(
                        out=tm[:], in0=ang[:], scalar1=phase, scalar2=1.0 / (2 * math.pi), op0=ALU.add, op1=ALU.mult),
                        reads=[angR], writes=[tmR])
                    op(DVE, lambda: nc.vector.tensor_copy(out=tmi[:], in_=tm[:]), reads=[tmR], writes=[tmiR])
                    op(DVE, lambda: nc.vector.tensor_copy(out=tm[:], in_=tmi[:]), reads=[tmiR], writes=[tmR])
                    yield
                    op(DVE, lambda: nc.vector.scalar_tensor_tensor(out=tm[:], in0=tm[:], scalar=-2 * math.pi, in1=ang[:],
                                                                   op0=ALU.mult, op1=ALU.add),
                       reads=[tmR, angR], writes=[tmR])
                    op(DVE, lambda phase=phase: nc.vector.tensor_scalar(
                        out=tm[:], in0=tm[:], scalar1=phase, scalar2=-math.pi, op0=ALU.add, op1=ALU.max),
                        reads=[tmR], writes=[tmR])
                    op(DVE, lambda: nc.vector.tensor_scalar(out=tm[:], in0=tm[:], scalar1=math.pi, scalar2=None,
                                                            op0=ALU.min), reads=[tmR], writes=[tmR])
                    op(ACT, lambda dst=dst: nc.scalar.activation(out=dst[:], in_=tm[:], func=AF.Sin),
                       reads=[tmR], writes=[dstR])
                    yield
                op(DVE, lambda: nc.vector.tensor_scalar(out=tsi[:], in0=tsi[:], scalar1=vecs[:, V_SGN:V_SGN + 1],
                                                        scalar2=None, op0=ALU.mult), reads=[tsiR, cst], writes=[tsiR])
                dma(SP, s_tco, cos_d[:, c * RC:(c + 1) * RC], tco[:], reads=[tcoR], writes=[rotR])
                dma(SP, s_tsi, sin_d[:, c * RC:(c + 1) * RC], tsi[:], reads=[tsiR], writes=[rotR])

        interleave(mod_gen(), rot_gen(), 1)
        derive_mod()
        barrier()

    cvt_flip = [0]

    def convert(out_ap, in_ap, reads, writes):
        cvt_flip[0] ^= 1
        if cvt_flip[0]:
            return op(DVE, lambda: nc.vector.tensor_copy(out=out_ap, in_=in_ap), reads=reads, writes=writes)
        return op(ACT, lambda: nc.scalar.copy(out=out_ap, in_=in_ap), reads=reads, writes=writes)

    def seq_of_tok(t):
        for s in range(NS):
            if offs[s] <= t < offs[s] + seqs[s]:
                return s
        raise ValueError

    def rsqrt_act(out_ap, in_ap, reads, writes):
        op(ACT, lambda: nc.scalar.activation(out=out_ap, in_=in_ap, func=AF.Ln, bias=epsc[:, 0:1], scale=1.0),
           reads=reads + [cst], writes=writes)
        op(ACT, lambda: nc.scalar.activation(out=out_ap, in_=out_ap, func=AF.Exp, scale=-0.5),
           reads=writes, writes=writes)

    def norm_h(T, xt, xR, sq, sqR, rstd, rstdR, tmp, tmpR, hT, hR, Avec, shift_col, statb):
        for kc in range(8):
            op(ACT, lambda kc=kc: nc.scalar.activation(out=sq[:, kc, 0:T], in_=xt[:, kc, 0:T], func=AF.Square),
               reads=[xR], writes=[sqR])
            yield

        def mm():
            for kc in range(8):
                ins = nc.tensor.matmul(ps[statb][:, 0:T], lhsT=onesD[:], rhs=sq[:, kc, 0:T],
                                       start=(kc == 0), stop=(kc == 7))
            return ins
        op(PE, mm, reads=[sqR, cst], writes=[psR[statb]])
        yield
        rsqrt_act(rstd[:, 0:T], ps[statb][:, 0:T], [psR[statb]], [rstdR])
        yield
        for kc in range(8):
            k2 = kc % 2
            op(DVE, lambda kc=kc, k2=k2: nc.vector.scalar_tensor_tensor(
                out=tmp[k2][:, 0:T], in0=xt[:, kc, 0:T], scalar=Avec(kc), in1=rstd[:, 0:T],
                op0=ALU.mult, op1=ALU.mult), reads=[xR, rstdR, cst], writes=[tmpR[k2]])
            op(ACT, lambda kc=kc, k2=k2: nc.scalar.activation(
                out=hT[:, kc, 0:T], in_=tmp[k2][:, 0:T], func=AF.Identity, bias=shift_col(kc), scale=1.0),
                reads=[tmpR[k2], cst], writes=[hR])
            yield

    for l in range(L):
        x_in = xT_d if l == 0 else xB_d
        x_out = yT_d if l == L - 1 else xB_d
        last = (l == L - 1)
        with ExitStack() as stL:
            NBMAX = SMAX // 128
            KT = sb(stL, "KT", [128, SMAX], BF16)
            Vaug = sb(stL, "Vaug", [128, NBMAX, 2, 128], BF16)
            KTR, VR = Res(), Res()
            op(DVE, lambda: nc.vector.memset(Vaug[:, :, :, 64:128], 1.0), writes=[VR])
            w_in = sb(stL, "w_in", [128, 8, NIN], BF16)
            woA = sb(stL, "woA", [64, 8, D], BF16)
            woC = sb(stL, "woC", [128, 4, D], BF16)
            winR, woR = Res(), Res()
            with ExitStack() as stg:
                stage = [sb(stg, "stg%d" % i, [128, NIN], F32) for i in range(2)]
                stR = [Res(), Res()]
                stS = [newsem("stg%d" % i) for i in range(2)]
                for kc in range(8):
                    sl = kc % 2
                    dma(SP, stS[sl], stage[sl][:], win_d[l, kc * 128:(kc + 1) * 128, :], writes=[stR[sl]])
                    convert(w_in[:, kc, 0:512].rearrange("p (m g d) -> p m g d", m=4, g=2),
                            stage[sl][:, 0:512].rearrange("p (g m d) -> p m g d", g=2, m=4), [stR[sl]], [winR])
                    convert(w_in[:, kc, 512:NIN], stage[sl][:, 512:NIN], [stR[sl]], [winR])
                for j in range(4):
                    sl = j % 2
                    dma(SP, stS[sl], stage[sl][0:64, 0:2 * D].rearrange("p (a n) -> p a n", a=2),
                        wout_d[l, j * 128:(j + 1) * 128, :].rearrange("(a p) n -> p a n", p=64), writes=[stR[sl]])
                    convert(woA[:, 2 * j:2 * j + 2, :], stage[sl][0:64, 0:2 * D].rearrange("p (a n) -> p a n", a=2),
                            [stR[sl]], [woR])
                for j in range(2):
                    sl = j % 2
                    dma(SP, stS[sl], stage[sl][:, 0:2 * D].rearrange("p (a n) -> p a n", a=2),
                        wout_d[l, 512 + j * 256:512 + (j + 1) * 256, :].rearrange("(a p) n -> p a n", p=128),
                        writes=[stR[sl]])
                    convert(woC[:, 2 * j:2 * j + 2, :], stage[sl][:, 0:2 * D].rearrange("p (a n) -> p a n", a=2),
                            [stR[sl]], [woR])
                barrier()
            for s in range(NS):
                S = seqs[s]
                base = offs[s]
                NB = S // 128
                with ExitStack() as st1:
                    T = 256
                    TH = T + 2
                    xt = [sb(st1, "xt%d" % i, [128, 8, TH], F32) for i in range(2)]
                    hT = [sb(st1, "hT%d" % i, [128, 8, TH], BF16) for i in range(2)]
                    sq = sb(st1, "sq", [128, 8, TH], BF16)
                    rstd = sb(st1, "rstd", [128, TH], F32)
                    tmp = [sb(st1, "tmp%d" % i, [128, TH], F32) for i in range(2)]
                    cosb = [sb(st1, "cosb%d" % i, [128, T], F32) for i in range(3)]
                    sinb = [sb(st1, "sinb%d" % i, [128, T], F32) for i in range(3)]
                    qf = [sb(st1, "qf%d" % i, [128, T], F32) for i in range(2)]
                    sq2 = [sb(st1, "sq2%d" % i, [128, T], BF16) for i in range(2)]
                    rh = [sb(st1, "rh%d" % i, [128, T], F32) for i in range(2)]
                    ta = [sb(st1, "ta%d" % i, [128, T], F32) for i in range(2)]
                    tb_ = [sb(st1, "tb%d" % i, [128, T], F32) for i in range(2)]
                    qTt = [sb(st1, "qTt%d" % i, [128, 4, T], BF16) for i in range(2)]
                    Bsb = sb(st1, "Bsb", [128, 4, T], F32)
                    Csb = sb(st1, "Csb", [128, 4, TH], F32)
                    zb = sb(st1, "zb", [128, 4, TH], F32)
                    yb = [sb(st1, "yb%d" % i, [128, T], F32) for i in range(2)]
                    y0b = sb(st1, "y0b", [128, T], F32)
                    y0R = Res()
                    cvT = [sb(st1, "cvT%d" % i, [128, 4, T], BF16) for i in range(2)]
                    xR = [Res(), Res()]
                    hR = [Res(), Res()]
                    sqR, rstdR = Res(), Res()
                    tmpR = [Res(), Res()]
                    tabR = [Res(), Res(), Res()]
                    qfR, sq2R, rhR, taR, tbR = ([Res(), Res()] for _ in range(5))
                    qTR = [Res(), Res()]
                    BR = [Res() for _ in range(4)]
                    CR = [Res() for _ in range(4)]
                    zR = [Res() for _ in range(4)]
                    ybR = [Res(), Res()]
                    cvR = [Res(), Res()]
                    xS = [newsem("x1_%d" % i) for i in range(2)]
                    tS = [newsem("t1_%d" % i) for i in range(3)]
                    qS_ = [newsem("q1_%d" % i) for i in range(2)]
                    cS_ = [newsem("c1_%d" % i) for i in range(2)]
                    scrR = Res()
                    for i in range(2):
                        op(DVE, lambda i=i: nc.vector.memset(xt[i][:], 0.0), writes=[xR[i]])
                    NT = S // T
                    xsrc = x_in.rearrange("(kc p) t -> p kc t", p=128)
                    pp = [0]
                    qk = [0]
                    wc = V_CONV + l * 12

                    def load1(i):
                        if i >= NT:
                            return
                        sl = i % 2
                        s3 = i % 3
                        t0 = i * T
                        lo = max(t0 - 1, 0)
                        hi = min(t0 + T + 1, S)
                        c0 = lo - (t0 - 1)
                        dma(SP, xS[sl], xt[sl][:, :, c0:c0 + (hi - lo)], xsrc[:, :, base + lo:base + hi],
                            writes=[xR[sl]])
                        dma(SP, tS[s3], cosb[s3][:], cos_d[:, t0:t0 + T], reads=[rotR], writes=[tabR[s3]])
                        dma(SP, tS[s3], sinb[s3][:], sin_d[:, t0:t0 + T], reads=[rotR], writes=[tabR[s3]])

                    def prep_gen(i):
                        sl = i % 2
                        yield from norm_h(TH, xt[sl], xR[sl], sq, sqR, rstd, rstdR, tmp, tmpR, hT[sl], hR[sl],
                                          lambda kc: A1[:, l, s, kc:kc + 1], lambda kc: modT[:, l, kc, s:s + 1], 4)

                    def proj_gen(i):
                        sl = i % 2
                        s3 = i % 3
                        t0 = i * T
                        deferred = []
                        load1(i + 2)

                        def main_mm(m, b):
                            full = m >= 10
                            c_lo, c_n = (0, TH) if full else (1, T)

                            def mmp():
                                for kc in range(8):
                                    ins = nc.tensor.matmul(ps[b][:, 0:c_n], lhsT=w_in[:, kc, m * 128:(m + 1) * 128],
                                                           rhs=hT[sl][:, kc, c_lo:c_lo + c_n],
                                                           start=(kc == 0), stop=(kc == 7))
                                return ins
                            op(PE, mmp, reads=[hR[sl], winR], writes=[psR[b]])

                        def qk_post(m, r):
                            gcol = (V_QG if m < 4 else V_KG) + l
                            gscol = (V_QGS if m < 4 else V_KGS) + l
                            op(PE, lambda: nc.tensor.matmul(ps[5][:, 0:T], lhsT=blk64[:], rhs=sq2[r][:],
                                                            start=True, stop=True),
                               reads=[sq2R[r], cst], writes=[psR[5]])
                            op(PE, lambda: nc.tensor.matmul(ps[6][:, 0:T], lhsT=permF[:], rhs=qf[r][:],
                                                            start=True, stop=True),
                               reads=[qfR[r], cst], writes=[psR[6]])
                            rsqrt_act(rh[r][:], ps[5][:, 0:T], [psR[5]], [rhR[r]])
                            op(DVE, lambda: nc.vector.scalar_tensor_tensor(
                                out=ta[r][:], in0=qf[r][:], scalar=vecs[:, gcol:gcol + 1], in1=cosb[s3][:],
                                op0=ALU.mult, op1=ALU.mult), reads=[qfR[r], tabR[s3], cst], writes=[taR[r]])
                            op(DVE, lambda: nc.vector.scalar_tensor_tensor(
                                out=tb_[r][:], in0=ps[6][:, 0:T], scalar=vecs[:, gscol:gscol + 1], in1=sinb[s3][:],
                                op0=ALU.mult, op1=ALU.mult), reads=[psR[6], tabR[s3], cst], writes=[tbR[r]])
                            op(POOL, lambda: nc.gpsimd.tensor_tensor(out=ta[r][:], in0=ta[r][:], in1=tb_[r][:],
                                                                     op=ALU.add),
                               reads=[taR[r], tbR[r]], writes=[taR[r]])
                            if m < 4:
                                op(DVE, lambda: nc.vector.tensor_tensor(
                                    out=qTt[sl][:, m, :], in0=ta[r][:], in1=rh[r][:], op=ALU.mult),
                                    reads=[taR[r], rhR[r]], writes=[qTR[sl]])
                            else:
                                op(DVE, lambda: nc.vector.tensor_tensor(
                                    out=KT[:, t0:t0 + T], in0=ta[r][:], in1=rh[r][:], op=ALU.mult),
                                    reads=[taR[r], rhR[r]], writes=[KTR])

                        order = [0, 1, 2, 3, 4] + list(range(10, 14)) + list(range(6, 10)) + list(range(14, 18))
                        for n, m in enumerate(order):
                            b = pp[0] % 4
                            pp[0] += 1
                            main_mm(m, b)
                            if len(deferred) >= 2:
                                deferred.pop(0)()
                            if m < 5:
                                r = qk[0] % 2
                                qk[0] += 1
                                op(ACT, lambda: nc.scalar.copy(out=qf[r][:], in_=ps[b][:, 0:T]),
                                   reads=[psR[b]], writes=[qfR[r]])
                                op(ACT, lambda: nc.scalar.activation(out=sq2[r][:], in_=ps[b][:, 0:T],
                                                                     func=AF.Square),
                                   reads=[psR[b]], writes=[sq2R[r]])
                                deferred.append(lambda m=m, r=r: qk_post(m, r))
                            elif m < 10:
                                cc = m - 6
                                op(ACT, lambda: nc.scalar.copy(out=Bsb[:, cc, :], in_=ps[b][:, 0:T]),
                                   reads=[psR[b]], writes=[BR[cc]])
                            elif m < 14:
                                cc = m - 10
                                op(ACT, lambda: nc.scalar.copy(out=Csb[:, cc, :], in_=ps[b][:, 0:TH]),
                                   reads=[psR[b]], writes=[CR[cc]])
                            else:
                                cc = m - 14
                                op(DVE, lambda: nc.vector.tensor_tensor(
                                    out=zb[:, cc, :], in0=Csb[:, cc, :], in1=ps[b][:, 0:TH], op=ALU.mult),
                                    reads=[psR[b], CR[cc]], writes=[zR[cc]])
                                if i == 0:
                                    op(DVE, lambda: nc.vector.memset(zb[:, cc, 0:1], 0.0), writes=[zR[cc]])
                                if i == NT - 1:
                                    op(DVE, lambda: nc.vector.memset(zb[:, cc, TH - 1:TH], 0.0), writes=[zR[cc]])
                                y2 = cc % 2
                                op(DVE, lambda: nc.vector.scalar_tensor_tensor(
                                    out=yb[y2][:], in0=ps[b][:, 1:1 + T], scalar=vecs[:, wc + 4 + cc:wc + 5 + cc],
                                    in1=Csb[:, cc, 1:1 + T], op0=ALU.mult, op1=ALU.mult),
                                    reads=[psR[b], CR[cc], cst], writes=[ybR[y2]])

                                def conv_rest(cc=cc, y2=y2):
                                    for (zo, wo) in ((0, 0), (2, 8)):
                                        op(DVE, lambda: nc.vector.scalar_tensor_tensor(
                                            out=yb[y2][:], in0=zb[:, cc, zo:zo + T],
                                            scalar=vecs[:, wc + wo + cc:wc + wo + cc + 1], in1=yb[y2][:],
                                            op0=ALU.mult, op1=ALU.add), reads=[zR[cc], ybR[y2], cst],
                                            writes=[ybR[y2]])
                                    op(POOL, lambda: nc.gpsimd.tensor_tensor(
                                        out=cvT[sl][:, cc, :], in0=yb[y2][:], in1=Bsb[:, cc, :], op=ALU.mult),
                                        reads=[ybR[y2], BR[cc]], writes=[cvR[sl]])
                                deferred.append(conv_rest)
                            yield
                        while deferred:
                            deferred.pop(0)()
                        for tb in range(T // 128):
                            def mmv(tb=tb):
                                for kc in range(8):
                                    ins = nc.tensor.matmul(
                                        ps[7][:, tb * 128:(tb + 1) * 128],
                                        lhsT=hT[sl][:, kc, 1 + tb * 128:1 + (tb + 1) * 128],
                                        rhs=w_in[:, kc, 640:768], start=(kc == 0), stop=(kc == 7))
                                return ins
                            op(PE, mmv, reads=[hR[sl], winR], writes=[psR[7]])
                        for tb in range(T // 128):
                            blk = (t0 // 128) + tb
                            op(ACT, lambda tb=tb, blk=blk: nc.scalar.copy(
                                out=Vaug[:, blk, :, 0:64],
                                in_=ps[7][:, tb * 128:(tb + 1) * 128].rearrange("p (g d) -> p g d", g=2)),
                                reads=[psR[7]], writes=[VR])
                        yield
                        dma(SP, qS_[sl],
                            qS_d.rearrange("h d t -> (h d) t").rearrange("(c p) t -> p c t", p=128)[
                                :, :, base + t0:base + t0 + T],
                            qTt[sl][:], reads=[qTR[sl]], writes=[scrR])
                        dma(SP, cS_[sl], cS_d.rearrange("(c p) t -> p c t", p=128)[:, :, base + t0:base + t0 + T],
                            cvT[sl][:], reads=[cvR[sl]], writes=[scrR])

                    load1(0)
                    load1(1)
                    run_all(prep_gen(0))
                    for i in range(NT):
                        if i + 1 < NT:
                            interleave(proj_gen(i), prep_gen(i + 1), 2)
                        else:
                            run_all(proj_gen(i))
                    barrier()
                with ExitStack() as st2:
                    T = 512
                    q2 = [sb(st2, "q2_%d" % i, [128, 8, T], BF16) for i in range(2)]
                    c2 = [sb(st2, "c2_%d" % i, [128, 4, T], BF16) for i in range(2)]
                    x2 = [sb(st2, "x2_%d" % i, [128, 8, T], F32) for i in range(2)]
                    Pb = [sb(st2, "Pb%d" % i, [128, 2 * T], BF16) for i in range(3)]
                    rcb = [sb(st2, "rcb%d" % i, [128, T], F32) for i in range(2)]
                    attn = [sb(st2, "attn%d" % i, [64, 8, T], BF16) for i in range(2)]
                    q2R, c2R, x2R = [Res(), Res()], [Res(), Res()], [Res(), Res()]
                    PbR = [Res() for _ in range(3)]
                    rcR = [Res(), Res()]
                    attnR = [Res(), Res()]
                    q2S = [newsem("q2_%d" % i) for i in range(2)]
                    c2S = [newsem("c2_%d" % i) for i in range(2)]
                    x2S = [newsem("x2_%d" % i) for i in range(2)]
                    x1R = Res()
                    NT = S // T
                    xsrc = x_in.rearrange("(kc p) t -> p kc t", p=128)
                    xdst = xA_d.rearrange("(kc p) t -> p kc t", p=128)
                    qsrc = qS_d.rearrange("h d t -> (h d) t").rearrange("(m p) t -> p m t", p=128)
                    for i in range(2):
                        op(DVE, lambda i=i: nc.vector.memset(q2[i][:], 0.0), writes=[q2R[i]])
                    csrc = cS_d.rearrange("(c p) t -> p c t", p=128)
                    xo = [0]
                    NBP = NB // 2

                    def load2(i):
                        if i >= NT:
                            return
                        sl = i % 2
                        t0 = base + i * T
                        dma(SP, q2S[sl], q2[sl][0:64, 0:4, :], qsrc[0:64, :, t0:t0 + T], reads=[scrR],
                            writes=[q2R[sl]])
                        dma(SP, q2S[sl], q2[sl][64:128, 4:8, :], qsrc[64:128, :, t0:t0 + T], reads=[scrR],
                            writes=[q2R[sl]])
                        dma(SP, c2S[sl], c2[sl][:], csrc[:, :, t0:t0 + T], reads=[scrR], writes=[c2R[sl]])
                        dma(SP, x2S[sl], x2[sl][:], xsrc[:, :, t0:t0 + T], writes=[x2R[sl]])

                    def attn_gen(i):
                        sl = i % 2
                        t0 = base + i * T
                        groups = [(h, bp) for h in range(8) for bp in range(NBP)]
                        NG = len(groups)

                        def QK2(g):
                            h, bp = groups[g]
                            kv = h // 4
                            b = g % 2

                            def f():
                                for j in range(2):
                                    blk = 2 * bp + j
                                    ins = nc.tensor.matmul(psP[b][:, j * 512:(j + 1) * 512],
                                                           lhsT=KT[:, blk * 128:(blk + 1) * 128],
                                                           rhs=q2[sl][:, h, :], start=True, stop=True)
                                return ins
                            op(PE, f, reads=[KTR, q2R[sl]], writes=[psR[2 * b], psR[2 * b + 1]])

                        def EXP2(g):
                            b = g % 2
                            p3 = g % 3
                            op(ACT, lambda: nc.scalar.activation(out=Pb[p3][:], in_=psP[b][:, :], func=AF.Exp,
                                                                 scale=0.125),
                               reads=[psR[2 * b], psR[2 * b + 1]], writes=[PbR[p3]])

                        def PV2(g):
                            h, bp = groups[g]
                            kv = h // 4
                            p3 = g % 3
                            o = 4 + (h % 2)

                            def f():
                                for j in range(2):
                                    blk = 2 * bp + j
                                    ins = nc.tensor.matmul(ps[o][:, 0:T], lhsT=Vaug[:, blk, kv, :],
                                                           rhs=Pb[p3][:, j * 512:(j + 1) * 512],
                                                           start=(blk == 0), stop=(blk == NB - 1))
                                return ins
                            op(PE, f, reads=[VR, PbR[p3]], writes=[psR[o]])
                            if bp == NBP - 1:
                                r = h % 2
                                op(DVE, lambda: nc.vector.reciprocal(out=rcb[r][64:128, :], in_=ps[o][64:128, 0:T]),
                                   reads=[psR[o]], writes=[rcR[r]])
                                op(DVE, lambda: nc.vector.tensor_tensor(out=attn[sl][0:64, h, :], in0=ps[o][0:64, 0:T],
                                                                        in1=rcb[r][64:128, :], op=ALU.mult),
                                   reads=[psR[o], rcR[r]], writes=[attnR[sl]])
                        QK2(0)
                        if NG > 1:
                            QK2(1)
                        for g in range(NG):
                            EXP2(g)
                            if g + 2 < NG:
                                QK2(g + 2)
                            PV2(g)
                            if g == NG // 2:
                                load2(i + 1)
                            yield

                    def oproj_gen(i):
                        sl = i % 2
                        t0 = base + i * T
                        for m in range(8):
                            b = 6 + (xo[0] % 2)
                            xo[0] += 1

                            def mmo(m=m, b=b):
                                for h in range(8):
                                    nc.tensor.matmul(ps[b][:, 0:T], lhsT=woA[0:64, h, m * 128:(m + 1) * 128],
                                                     rhs=attn[sl][0:64, h, :], start=(h == 0), stop=False)
                                for cc in range(4):
                                    ins = nc.tensor.matmul(ps[b][:, 0:T], lhsT=woC[:, cc, m * 128:(m + 1) * 128],
                                                           rhs=c2[sl][:, cc, :], start=False, stop=(cc == 3))
                                return ins
                            op(PE, mmo, reads=[attnR[sl], c2R[sl], woR], writes=[psR[b]])
                            op(DVE, lambda m=m, b=b: nc.vector.scalar_tensor_tensor(
                                out=x2[sl][:, m, :], in0=ps[b][:, 0:T], scalar=modT[:, l, 16 + m, s:s + 1],
                                in1=x2[sl][:, m, :], op0=ALU.mult, op1=ALU.add),
                                reads=[psR[b], x2R[sl], cst], writes=[x2R[sl]])
                            yield
                        dma(SP, x2S[sl], xdst[:, :, t0:t0 + T], x2[sl][:], reads=[x2R[sl]], writes=[x1R])

                    pending = None
                    load2(0)
                    for i in range(NT):
                        cnt = 0
                        for _ in attn_gen(i):
                            cnt += 1
                            if pending is not None and cnt % 4 == 0:
                                if next(pending, "end") == "end":
                                    pending = None
                        if pending is not None:
                            for _ in pending:
                                pass
                        pending = oproj_gen(i)
                    for _ in pending:
                        pass
                    barrier()
            barrier()
        with ExitStack() as st3:
            T = 256
            wg = sb(st3, "wg", [128, 8, DFF], BF16)
            wu = sb(st3, "wu", [128, 8, DFF], BF16)
            wd = sb(st3, "wd", [128, FC, D], BF16)
            wgR, wuR, wdR = Res(), Res(), Res()
            HW = DFF // 2
            stage = [sb(st3, "stgf%d" % i, [128, HW], F32) for i in range(3)]
            stR = [Res(), Res(), Res()]
            stS = [newsem("stgf%d" % i) for i in range(3)]

            def wload_gen():
                j = 0
                for (wsrc, wdst, wr) in ((wg_d, wg, wgR), (wu_d, wu, wuR)):
                    for kc in range(8):
                        for hf in range(2):
                            sl = j % 3
                            j += 1
                            dma(SP, stS[sl], stage[sl][:], wsrc[l, kc * 128:(kc + 1) * 128, hf * HW:(hf + 1) * HW],
                                writes=[stR[sl]])
                            convert(wdst[:, kc, hf * HW:(hf + 1) * HW], stage[sl][:], [stR[sl]], [wr])
                        yield
                for jj in range(FC):
                    sl = j % 3
                    j += 1
                    dma(SP, stS[sl], stage[sl][:, 0:D], wd_d[l, jj * 128:(jj + 1) * 128, :], writes=[stR[sl]])
                    convert(wd[:, jj, :], stage[sl][:, 0:D], [stR[sl]], [wdR])
                    if jj % 2 == 1:
                        yield
            x3 = [sb(st3, "x3_%d" % i, [128, 8, T], F32) for i in range(3)]
            h3 = [sb(st3, "h3_%d" % i, [128, 8, T], BF16) for i in range(2)]
            sq = sb(st3, "sq3", [128, 8, T], BF16)
            rstd = sb(st3, "rstd3", [128, T], F32)
            tmp = [sb(st3, "tmp3_%d" % i, [128, T], F32) for i in range(2)]
            sg = [sb(st3, "sg%d" % i, [128, T], F32) for i in range(2)]
            aT = sb(st3, "aT", [128, FC, T], BF16)
            x3R, h3R = [Res(), Res(), Res()], [Res(), Res()]
            sqR, rstdR = Res(), Res()
            tmpR = [Res(), Res()]
            sgR = [Res(), Res()]
            aR = Res()
            x3S = [newsem("x3_%d" % i) for i in range(3)]
            outR = Res()
            xsrc = xA_d.rearrange("(kc p) t -> p kc t", p=128)
            xdst = x_out.rearrange("(kc p) t -> p kc t", p=128)
            NT = NTOK // T
            sqf, rstdf = sq, rstd
            sqfR, rstdfR = sqR, rstdR
            dn = [0]

            def load3(i):
                if i < NT:
                    xs = i % 3
                    t0 = i * T
                    dma(SP, x3S[xs], x3[xs][:], xsrc[:, :, t0:t0 + T], writes=[x3R[xs]])
                yield

            def prep3(i):
                sl = i % 2
                xs = i % 3
                t0 = i * T
                s = seq_of_tok(t0)
                yield from norm_h(T, x3[xs], x3R[xs], sq, sqR, rstd, rstdR, tmp, tmpR, h3[sl], h3R[sl],
                                  lambda kc: A2[:, l, s, kc:kc + 1], lambda kc: modT[:, l, 24 + kc, s:s + 1], 4)

            def ffn3(i):
                sl = i % 2
                xs = i % 3
                t0 = i * T
                s = seq_of_tok(t0)
                for j in range(FC):
                    bg = j % 2
                    bu = 2 + (j % 2)

                    def mmg():
                        for kc in range(8):
                            ins = nc.tensor.matmul(ps[bg][:, 0:T], lhsT=wg[:, kc, j * 128:(j + 1) * 128],
                                                   rhs=h3[sl][:, kc, :], start=(kc == 0), stop=(kc == 7))
                        return ins

                    def mmu():
                        for kc in range(8):
                            ins = nc.tensor.matmul(ps[bu][:, 0:T], lhsT=wu[:, kc, j * 128:(j + 1) * 128],
                                                   rhs=h3[sl][:, kc, :], start=(kc == 0), stop=(kc == 7))
                        return ins
                    op(PE, mmg, reads=[h3R[sl], wgR], writes=[psR[bg]])
                    op(PE, mmu, reads=[h3R[sl], wuR], writes=[psR[bu]])
                    op(ACT, lambda: nc.scalar.activation(out=sg[bg][:], in_=ps[bg][:, 0:T], func=AF.Silu),
                       reads=[psR[bg]], writes=[sgR[bg]])
                    op(DVE, lambda: nc.vector.tensor_tensor(
                        out=aT[:, j, :], in0=sg[bg][:], in1=ps[bu][:, 0:T], op=ALU.mult),
                        reads=[sgR[bg], psR[bu]], writes=[aR])
                    if j % 2 == 1:
                        yield
                for m in range(8):
                    b = 5 + (dn[0] % 2)
                    dn[0] += 1

                    def mmd():
                        for j in range(FC):
                            ins = nc.tensor.matmul(ps[b][:, 0:T], lhsT=wd[:, j, m * 128:(m + 1) * 128],
                                                   rhs=aT[:, j, :], start=(j == 0), stop=(j == FC - 1))
                        return ins
                    op(PE, mmd, reads=[aR, wdR], writes=[psR[b]])
                    op(DVE, lambda: nc.vector.scalar_tensor_tensor(
                        out=x3[xs][:, m, :], in0=ps[b][:, 0:T], scalar=modT[:, l, 40 + m, s:s + 1],
                        in1=x3[xs][:, m, :], op0=ALU.mult, op1=ALU.add),
                        reads=[psR[b], x3R[xs], cst], writes=[x3R[xs]])
                    yield

            def fin3(i):
                sl = i % 2
                xs = i % 3
                t0 = i * T
                if last:
                    for kc in range(8):
                        op(ACT, lambda kc=kc: nc.scalar.activation(out=sqf[:, kc, :], in_=x3[xs][:, kc, :],
                                                                   func=AF.Square), reads=[x3R[xs]], writes=[sqfR])
                        if kc % 2 == 1:
                            yield

                    def mmf():
                        for kc in range(8):
                            ins = nc.tensor.matmul(ps[7][:, 0:T], lhsT=onesD[:], rhs=sqf[:, kc, :],
                                                   start=(kc == 0), stop=(kc == 7))
                        return ins
                    op(PE, mmf, reads=[sqfR, cst], writes=[psR[7]])
                    yield
                    rsqrt_act(rstdf[:], ps[7][:, 0:T], [psR[7]], [rstdfR])
                    yield
                    for kc in range(8):
                        op(DVE, lambda kc=kc: nc.vector.scalar_tensor_tensor(
                            out=x3[xs][:, kc, :], in0=x3[xs][:, kc, :], scalar=vecs[:, V_GFIN + kc:V_GFIN + kc + 1],
                            in1=rstdf[:], op0=ALU.mult, op1=ALU.mult), reads=[x3R[xs], rstdfR, cst],
                            writes=[x3R[xs]])
                        if kc % 2 == 1:
                            yield
                dma(SP, x3S[xs], xdst[:, :, t0:t0 + T], x3[xs][:], reads=[x3R[xs]], writes=[outR])

            def chain(*gens):
                for g in gens:
                    yield from g

            run_all(load3(0))
            run_all(load3(1))
            interleave(wload_gen(), prep3(0), 1)
            for i in range(NT):
                side = []
                if i >= 1:
                    side.append(fin3(i - 1))
                side.append(load3(i + 2))
                if i + 1 < NT:
                    side.append(prep3(i + 1))
                interleave(ffn3(i), chain(*side), 2)
            run_all(fin3(NT - 1))
            barrier()
    barrier()
    es.close()
    return nc


def _pack_vecs(c_list, b_mod, g_mix, g_ffn, g_final, q_gain, k_gain, conv_w):
    NS = len(c_list)
    v = np.zeros((128, NV), np.float32)
    cT = np.stack(c_list, 0).reshape(NS, 8, 128).transpose(2, 1, 0)
    v[:, V_CT:V_CT + 8 * NS] = cT.reshape(128, 8 * NS)
    v[:, V_BMOD:V_BMOD + L * 48] = b_mod.reshape(L, 48, 128).transpose(2, 0, 1).reshape(128, L * 48)
    v[:, V_GMIX:V_GMIX + L * 8] = g_mix.reshape(L, 8, 128).transpose(2, 0, 1).reshape(128, L * 8)
    v[:, V_GFFN:V_GFFN + L * 8] = g_ffn.reshape(L, 8, 128).transpose(2, 0, 1).reshape(128, L * 8)
    v[:, V_GFIN:V_GFIN + 8] = g_final.reshape(8, 128).T
    p = np.arange(128)
    d = p % 64
    for l in range(L):
        v[:, V_QG + l] = q_gain[l][d]
        v[:, V_QGS + l] = q_gain[l][d ^ 1]
        v[:, V_KG + l] = k_gain[l][d]
        v[:, V_KGS + l] = k_gain[l][d ^ 1]
    v[:, V_CONV:V_CONV + L * 12] = conv_w.reshape(L, 3, 4, 128).transpose(3, 0, 1, 2).reshape(128, L * 12)
    pair = d // 2
    v[:, V_JF] = (pair % 16).astype(np.float32)
    v[:, V_AXIS] = (pair // 16).astype(np.float32)
    v[:, V_SGN] = np.where(d % 2 == 0, -1.0, 1.0).astype(np.float32)
    return v


def _consts(SMAX):
    t = np.arange(SMAX)
    pos = np.stack([(t // 64), (t % 64)], 0).astype(np.float32)
    pos = np.ascontiguousarray(np.broadcast_to(pos[None], (128, 2, SMAX)))
    perm = np.zeros((128, 128), np.float32)
    perm[np.arange(128) ^ 1, np.arange(128)] = 1.0
    return pos, perm


def run(seqs, x_list, c_lists, w):
    ncores = len(x_list)
    nc = build(seqs)
    pos, perm = _consts(max(seqs))
    in_maps = []
    for c in range(ncores):
        xT = np.ascontiguousarray(np.concatenate([np.asarray(x).T for x in x_list[c]], axis=1))
        vecs = _pack_vecs(c_lists[c], w["b_mod"], w["g_mix"], w["g_ffn"], w["g_final"], w["q_gain"], w["k_gain"],
                          w["conv_w"])
        in_maps.append({"xT": xT, "vecs": vecs, "pos": pos, "perm": perm, "w_mod": w["w_mod"], "w_in": w["w_in"],
                        "w_out": w["w_out"], "w_gate": w["w_gate"], "w_up": w["w_up"], "w_down": w["w_down"]})
    res = run_bass_kernel_spmd(nc, in_maps, core_ids=list(range(ncores)))
    outs = []
    for c in range(ncores):
        yT = res.results[c]["yT"]
        o = []
        off = 0
        for S in seqs:
            o.append(np.ascontiguousarray(yT[:, off:off + S].T))
            off += S
        outs.append(o)
    return outs


def kernel(x_prompt, x_sample, c_prompt, c_sample, w_mod, b_mod, g_mix, w_in, q_gain, k_gain, conv_w, w_out,
           g_ffn, w_gate, w_up, w_down, g_final):
    f = lambda a: np.ascontiguousarray(np.asarray(a, dtype=np.float32))
    x_prompt, x_sample, c_prompt, c_sample = f(x_prompt), f(x_sample), f(c_prompt), f(c_sample)
    w = {k: f(v) for k, v in dict(w_mod=w_mod, b_mod=b_mod, g_mix=g_mix, w_in=w_in, q_gain=q_gain, k_gain=k_gain,
                                  conv_w=conv_w, w_out=w_out, g_ffn=g_ffn, w_gate=w_gate, w_up=w_up, w_down=w_down,
                                  g_final=g_final).items()}
    seqs = [x_prompt.shape[1], x_sample.shape[1], x_sample.shape[1]]
    x_list = [[x_prompt[c], x_sample[2 * c], x_sample[2 * c + 1]] for c in range(NCORES)]
    c_lists = [[c_prompt[c], c_sample[2 * c], c_sample[2 * c + 1]] for c in range(NCORES)]
    outs = run(seqs, x_list, c_lists, w)
    y_prompt = np.stack([outs[c][0] for c in range(NCORES)], 0)
    y_sample = np.stack([outs[c // 2][1 + (c % 2)] for c in range(2 * NCORES)], 0)
    return (y_prompt, y_sample)
```
